# Optimizing a Trainium2 kernel written in Bass

```python
import math
import jax, jax.numpy as jnp
from jax import lax
import numpy as np

D_MODEL = 1024
BATCH = 32
SEQ = 2048
DEPTH = 1

CTX_LEN = 256
GRID_W = 64

N_ATTN_HEADS = 16
QK_NOPE_DIM = 64
QK_ROPE_DIM = 32
V_HEAD_DIM = 64
Q_LORA_RANK = 384
KV_LORA_RANK = 256
ROPE_THETA = 10000.0
Q_BLOCK = 128
ATTN_SCALE = (QK_NOPE_DIM + QK_ROPE_DIM) ** -0.5

N_SSD_HEADS = 16
SSD_HEAD_DIM = 64
SSD_GROUPS = 2
HEADS_PER_GROUP = N_SSD_HEADS // SSD_GROUPS
SSD_STATE = 128
SSD_CONV = 5
SSD_CHUNK = 128
D_INNER = N_SSD_HEADS * SSD_HEAD_DIM
GN = SSD_GROUPS * SSD_STATE
XBC_WIDTH = D_INNER + 2 * GN

ATTN_WIDTH = N_ATTN_HEADS * V_HEAD_DIM
MIX_WIDTH = ATTN_WIDTH + D_INNER

D_FF = 2816
FFN_CONV = 3

N_MOD = 6
EPS = 1e-6

IN_SPLITS = (Q_LORA_RANK, KV_LORA_RANK, QK_ROPE_DIM, D_INNER, XBC_WIDTH, 2 * N_SSD_HEADS)
IN_WIDTH = sum(IN_SPLITS)
IN_OFFSETS = tuple(int(o) for o in np.cumsum(IN_SPLITS)[:-1])

kernel_name = "hybrid_mla_ssd_diffusion_layer"


def rms_norm(x, w):
    xf = x.astype(jnp.float32)
    y = xf * lax.rsqrt(jnp.mean(xf * xf, axis=-1, keepdims=True) + EPS)
    return (y * w.astype(jnp.float32)).astype(x.dtype)


def modulate(h, shift, scale):
    return h * (1 + scale) + shift


def in_split(h):
    return jnp.split(h, IN_OFFSETS, axis=-1)


def depthwise_conv(x, w, b):
    k = w.shape[0]
    out = lax.conv_general_dilated(
        x, w[:, None, :].astype(x.dtype), window_strides=(1,), padding=((k // 2, k // 2),),
        dimension_numbers=("NWC", "WIO", "NWC"), feature_group_count=x.shape[-1])
    return out + b


def axial_rope_tables(seq_len):
    n_rows = seq_len // GRID_W
    row = jnp.repeat(jnp.arange(n_rows), GRID_W).astype(jnp.float32)
    col = jnp.tile(jnp.arange(GRID_W), n_rows).astype(jnp.float32)
    axis_dim = QK_ROPE_DIM // 2
    inv_freq = ROPE_THETA ** (-jnp.arange(0, axis_dim, 2, dtype=jnp.float32) / axis_dim)
    ang_r = row[:, None] * inv_freq
    ang_c = col[:, None] * inv_freq
    ang = jnp.concatenate([ang_r, ang_r, ang_c, ang_c], axis=-1)
    return jnp.cos(ang), jnp.sin(ang)


def rotate_half_axial(x):
    def rh(t):
        a, b = jnp.split(t, 2, axis=-1)
        return jnp.concatenate([-b, a], axis=-1)
    xr, xc = jnp.split(x, 2, axis=-1)
    return jnp.concatenate([rh(xr), rh(xc)], axis=-1)


def apply_rope(x, cos, sin):
    xf = x.astype(jnp.float32)
    return (xf * cos + rotate_half_axial(xf) * sin).astype(x.dtype)


def mla_queries(cq, q_norm_w, w_q_up):
    b, l, _ = cq.shape
    q = (rms_norm(cq, q_norm_w) @ w_q_up).reshape(b, l, N_ATTN_HEADS, QK_NOPE_DIM + QK_ROPE_DIM)
    return q[..., :QK_NOPE_DIM], q[..., QK_NOPE_DIM:]


def mla_keys_values(ckv, kv_norm_w, w_kv_up):
    b, l, _ = ckv.shape
    kv = (rms_norm(ckv, kv_norm_w) @ w_kv_up).reshape(b, l, N_ATTN_HEADS, QK_NOPE_DIM + V_HEAD_DIM)
    return kv[..., :QK_NOPE_DIM], kv[..., QK_NOPE_DIM:]


def attend(q_nope, q_rope, k_nope, k_rope, v):
    s = (jnp.einsum("bqhd,bkhd->bhqk", q_nope, k_nope)
         + jnp.einsum("bqhr,bkr->bhqk", q_rope, k_rope)) * ATTN_SCALE
    p = jax.nn.softmax(s.astype(jnp.float32), axis=-1).astype(v.dtype)
    return jnp.einsum("bhqk,bkhd->bqhd", p, v)


def latent_attention(q_nope, q_rope, k_nope, k_rope, v):
    b, s, h, _ = q_nope.shape
    nb = s // Q_BLOCK

    def to_blocks(t):
        return jnp.moveaxis(t.reshape(b, nb, Q_BLOCK, *t.shape[2:]), 1, 0)

    out = lax.map(lambda qb: attend(qb[0], qb[1], k_nope, k_rope, v),
                  (to_blocks(q_nope), to_blocks(q_rope)))
    return jnp.moveaxis(out, 0, 1).reshape(b, s, h * V_HEAD_DIM)


def ssd_prepare(xbc, dt_raw, conv_w, conv_b, dt_bias):
    b, l, _ = xbc.shape
    xbc = jax.nn.silu(depthwise_conv(xbc, conv_w, conv_b))
    xs, bm, cm = jnp.split(xbc, [D_INNER, D_INNER + GN], axis=-1)
    xs = xs.reshape(b, l, SSD_GROUPS, HEADS_PER_GROUP, SSD_HEAD_DIM)
    bm = bm.reshape(b, l, SSD_GROUPS, SSD_STATE)
    cm = cm.reshape(b, l, SSD_GROUPS, SSD_STATE)
    dt = jax.nn.softplus(dt_raw.astype(jnp.float32).reshape(b, l, 2, SSD_GROUPS, HEADS_PER_GROUP)
                         + dt_bias.astype(jnp.float32).reshape(2, SSD_GROUPS, HEADS_PER_GROUP))
    return xs, bm, cm, dt


def segment_decay(cum):
    n = cum.shape[-1]
    diff = cum[..., :, None] - cum[..., None, :]
    mask = jnp.tril(jnp.ones((n, n), dtype=bool))
    return jnp.exp(jnp.where(mask, diff, -jnp.inf))


def ssd_states(xh, dt, A, bm, cm, init_state):
    b, l, g, e, p = xh.shape
    nc = l // SSD_CHUNK
    xd = (xh * dt[..., None]).reshape(b, nc, SSD_CHUNK, g, e, p)
    a_cum = jnp.cumsum(jnp.moveaxis((dt * A).reshape(b, nc, SSD_CHUNK, g, e), 2, -1), axis=-1)
    bc = bm.reshape(b, nc, SSD_CHUNK, g, -1)
    cc = cm.reshape(b, nc, SSD_CHUNK, g, -1)
    decay_to_end = jnp.exp(a_cum[..., -1:] - a_cum)
    chunk_states = jnp.einsum("bclgn,bcgel,bclgep->bcgepn", bc, decay_to_end, xd)
    states = jnp.concatenate([init_state[:, None], chunk_states], axis=1)
    chunk_cum = jnp.cumsum(jnp.pad(a_cum[..., -1], ((0, 0), (1, 0), (0, 0), (0, 0))), axis=1)
    decay_chunks = segment_decay(jnp.moveaxis(chunk_cum, 1, -1))
    states = jnp.einsum("bgezc,bcgepn->bzgepn", decay_chunks, states)
    return (xd, a_cum, bc, cc), states[:, :-1], states[:, -1]


def ssd_output(pieces, entering_states):
    xd, a_cum, bc, cc = pieces
    within = segment_decay(a_cum)
    cb = jnp.einsum("bclgn,bcsgn->bcgls", cc, bc)
    y_diag = jnp.einsum("bcgls,bcgels,bcsgep->bclgep", cb, within, xd)
    y_off = jnp.einsum("bclgn,bcgepn,bcgel->bclgep", cc, entering_states, jnp.exp(a_cum))
    b, nc, q, g, e, p = y_diag.shape
    return (y_diag + y_off).reshape(b, nc * q, g, e, p)


def ssd_finish(y, xs, z, d_skip, norm_w):
    b, l = y.shape[:2]
    y = y + d_skip.reshape(SSD_GROUPS, HEADS_PER_GROUP, 1) * xs
    y = y.reshape(b, l, D_INNER).astype(z.dtype)
    return rms_norm(y * jax.nn.silu(z), norm_w)


def conv_glu(h, w_up, conv_w, conv_b, w_down):
    gate, val = jnp.split(h @ w_up, 2, axis=-1)
    gate = depthwise_conv(gate, conv_w, conv_b)
    return (jax.nn.gelu(gate, approximate=False) * val) @ w_down


def setup_inputs(seed: int = 0) -> dict:
    key = jax.random.key(seed)
    ks = jax.random.split(key, 32)
    f32 = jnp.float32
    L = DEPTH

    def dense(k, shape, fan_in):
        return jax.random.normal(k, shape, f32) * fan_in ** -0.5

    def gain(k, shape):
        return 1.0 + 0.1 * jax.random.normal(k, shape, f32)

    def bias(k, shape):
        return 0.02 * jax.random.normal(k, shape, f32)

    dt0 = jnp.exp(jax.random.uniform(ks[15], (L, 2, N_SSD_HEADS), f32, math.log(1e-3), math.log(1e-1)))
    return {
        "x": jax.random.normal(ks[0], (BATCH, SEQ, D_MODEL), f32),
        "c": jax.random.normal(ks[1], (BATCH, D_MODEL), f32),
        "ctx": jax.random.normal(ks[2], (BATCH, CTX_LEN, D_MODEL), f32),
        "c_ctx": jax.random.normal(ks[3], (D_MODEL,), f32),
        "w_mod": dense(ks[4], (L, D_MODEL, N_MOD * D_MODEL), D_MODEL),
        "b_mod": bias(ks[5], (L, N_MOD * D_MODEL)),
        "mix_pre_norm": gain(ks[6], (L, D_MODEL)),
        "mix_post_norm": gain(ks[7], (L, D_MODEL)),
        "w_in": dense(ks[8], (L, D_MODEL, IN_WIDTH), D_MODEL),
        "q_norm": gain(ks[9], (L, Q_LORA_RANK)),
        "w_q_up": dense(ks[10], (L, Q_LORA_RANK, N_ATTN_HEADS * (QK_NOPE_DIM + QK_ROPE_DIM)), Q_LORA_RANK),
        "kv_norm": gain(ks[11], (L, KV_LORA_RANK)),
        "w_kv_up": dense(ks[12], (L, KV_LORA_RANK, N_ATTN_HEADS * (QK_NOPE_DIM + V_HEAD_DIM)), KV_LORA_RANK),
        "ssd_conv_w": dense(ks[13], (L, SSD_CONV, XBC_WIDTH), SSD_CONV),
        "ssd_conv_b": bias(ks[14], (L, XBC_WIDTH)),
        "ssd_a_log": jnp.log(jax.random.uniform(ks[16], (L, 2, N_SSD_HEADS), f32, 1.0, 16.0)),
        "ssd_dt_bias": dt0 + jnp.log(-jnp.expm1(-dt0)),
        "ssd_d": gain(ks[17], (L, N_SSD_HEADS)),
        "ssd_norm": gain(ks[18], (L, D_INNER)),
        "w_out": dense(ks[19], (L, MIX_WIDTH, D_MODEL), MIX_WIDTH),
        "ffn_pre_norm": gain(ks[20], (L, D_MODEL)),
        "ffn_post_norm": gain(ks[21], (L, D_MODEL)),
        "w_up": dense(ks[22], (L, D_MODEL, 2 * D_FF), D_MODEL),
        "ffn_conv_w": dense(ks[23], (L, FFN_CONV, D_FF), FFN_CONV),
        "ffn_conv_b": bias(ks[24], (L, D_FF)),
        "w_down": dense(ks[25], (L, D_FF, D_MODEL), D_FF),
    }


def reference(x, c, ctx, c_ctx, w_mod, b_mod, mix_pre_norm, mix_post_norm, w_in, q_norm, w_q_up,
              kv_norm, w_kv_up, ssd_conv_w, ssd_conv_b, ssd_a_log, ssd_dt_bias, ssd_d, ssd_norm, w_out,
              ffn_pre_norm, ffn_post_norm, w_up, ffn_conv_w, ffn_conv_b, w_down):
    bsz, seq, _ = x.shape
    cos, sin = axial_rope_tables(seq)
    cos_q, sin_q = cos[:, None, :], sin[:, None, :]

    def flip(t):
        return jnp.flip(t, axis=1)

    for l in range(DEPTH):
        last = l == DEPTH - 1
        mod_x = jnp.split((jax.nn.silu(c) @ w_mod[l] + b_mod[l])[:, None, :], N_MOD, axis=-1)
        mod_c = jnp.split((jax.nn.silu(c_ctx) @ w_mod[l] + b_mod[l])[None, None, :], N_MOD, axis=-1)

        cq_x, ckv_x, kr_x, z_x, xbc_x, dt_x = in_split(
            modulate(rms_norm(x, mix_pre_norm[l]), mod_x[0], mod_x[1]) @ w_in[l])
        cq_c, ckv_c, kr_c, z_c, xbc_c, dt_c = in_split(
            modulate(rms_norm(ctx, mix_pre_norm[l]), mod_c[0], mod_c[1]) @ w_in[l])

        q_nope_x, q_rope_x = mla_queries(cq_x, q_norm[l], w_q_up[l])
        q_rope_x = apply_rope(q_rope_x, cos_q, sin_q)
        k_nope_x, v_x = mla_keys_values(ckv_x, kv_norm[l], w_kv_up[l])
        k_rope_x = apply_rope(kr_x, cos, sin)
        k_nope_c, v_c = mla_keys_values(ckv_c, kv_norm[l], w_kv_up[l])
        attn_x = latent_attention(q_nope_x, q_rope_x,
                                  jnp.concatenate([k_nope_c, k_nope_x], axis=1),
                                  jnp.concatenate([kr_c, k_rope_x], axis=1),
                                  jnp.concatenate([v_c, v_x], axis=1))

        A = -jnp.exp(ssd_a_log[l].astype(jnp.float32)).reshape(2, SSD_GROUPS, HEADS_PER_GROUP)
        xs_x, b_x, c_x, dtv_x = ssd_prepare(xbc_x, dt_x, ssd_conv_w[l], ssd_conv_b[l], ssd_dt_bias[l])
        xs_c, b_c, c_c, dtv_c = ssd_prepare(xbc_c, dt_c, ssd_conv_w[l], ssd_conv_b[l], ssd_dt_bias[l])
        zero_state = jnp.zeros((bsz, SSD_GROUPS, HEADS_PER_GROUP, SSD_HEAD_DIM, SSD_STATE), xs_x.dtype)
        st_cf = ssd_states(xs_c, dtv_c[:, :, 0], A[0], b_c, c_c, zero_state)
        st_cb = ssd_states(flip(xs_c), flip(dtv_c[:, :, 1]), A[1], flip(b_c), flip(c_c), zero_state)
        st_xf = ssd_states(xs_x, dtv_x[:, :, 0], A[0], b_x, c_x, st_cf[2])
        st_xb = ssd_states(flip(xs_x), flip(dtv_x[:, :, 1]), A[1], flip(b_x), flip(c_x), st_cb[2])
        y_x = ssd_output(st_xf[0], st_xf[1]) + flip(ssd_output(st_xb[0], st_xb[1]))
        ssd_x = ssd_finish(y_x, xs_x, z_x, ssd_d[l], ssd_norm[l])

        mix_x = jnp.concatenate([attn_x, ssd_x], axis=-1) @ w_out[l]
        x = x + mod_x[2] * rms_norm(mix_x, mix_post_norm[l])

        if not last:
            q_nope_c, q_rope_c = mla_queries(cq_c, q_norm[l], w_q_up[l])
            attn_c = attend(q_nope_c, q_rope_c, k_nope_c, kr_c, v_c).reshape(bsz, -1, ATTN_WIDTH)
            y_c = ssd_output(st_cf[0], st_cf[1]) + flip(ssd_output(st_cb[0], st_cb[1]))
            ssd_c = ssd_finish(y_c, xs_c, z_c, ssd_d[l], ssd_norm[l])
            mix_c = jnp.concatenate([attn_c, ssd_c], axis=-1) @ w_out[l]
            ctx = ctx + mod_c[2] * rms_norm(mix_c, mix_post_norm[l])
            ffn_c = conv_glu(modulate(rms_norm(ctx, ffn_pre_norm[l]), mod_c[3], mod_c[4]),
                             w_up[l], ffn_conv_w[l], ffn_conv_b[l], w_down[l])
            ctx = ctx + mod_c[5] * rms_norm(ffn_c, ffn_post_norm[l])

        ffn_x = conv_glu(modulate(rms_norm(x, ffn_pre_norm[l]), mod_x[3], mod_x[4]),
                         w_up[l], ffn_conv_w[l], ffn_conv_b[l], w_down[l])
        x = x + mod_x[5] * rms_norm(ffn_x, ffn_post_norm[l])
    return x
```

```python
import contextlib
import numpy as np
import concourse.bass as bass
import concourse.mybir as mybir
from concourse.bass_utils import run_bass_kernel_spmd

F32 = mybir.dt.float32
BF16 = mybir.dt.bfloat16
AF = mybir.ActivationFunctionType
ALU = mybir.AluOpType

D = 1024
NH = 16
DFF = 2816
NFF = DFF // 128
INW = 3264
EPS = 1e-6
SCALE = 96 ** -0.5


class Buf:
    def __init__(self, t, name):
        self.t = t
        self.name = name
        self.lw = None
        self.rd = {}
        self.psum = False
        self.xacc = None

    def __getitem__(self, k):
        return self.t[k]


class Sched:
    def __init__(self, nc, es):
        self.nc = nc
        self.es = es
        self.eng = {'pe': nc.tensor, 'act': nc.scalar, 'dve': nc.vector, 'pool': nc.gpsimd, 'sp': nc.sync}
        self.sem = {e: es.enter_context(nc.semaphore('s_' + e)) for e in ['pe', 'act', 'dve', 'pool']}
        self.cnt = {e: 0 for e in self.sem}
        self.known = {e: {} for e in self.eng}
        self.dtot = {}
        self.dsem = {}
        self.free_sems = {'sp': [], 'pool': []}
        self.all_dsems = []
        self.nsem = 0

    def _wait(self, e, tok):
        sem, val, src = tok
        if src == 'dma':
            val = max(val, self.dtot[id(sem)])
        k = self.known[e]
        if k.get(id(sem), 0) >= val:
            return
        self.eng[e].wait_ge(sem, val)
        k[id(sem)] = val

    def _deps(self, e, reads, writes, is_dma=False):
        toks = []
        for b in reads:
            if b.lw is not None:
                toks.append((b.lw, 'raw'))
        for b in writes:
            if b.lw is not None:
                toks.append((b.lw, 'waw'))
            for r in b.rd.values():
                toks.append((r, 'war'))
        for tok, kind in toks:
            if (not is_dma) and tok[2] == e and e == 'pe':
                continue
            self._wait(e, tok)

    def _update(self, tok, reads, writes):
        for b in reads:
            b.rd[id(tok[0])] = tok
        for b in writes:
            b.lw = tok
            b.rd = {}

    def op(self, e, fn, reads=(), writes=(), sig=True):
        self._deps(e, reads, writes)
        if e != 'pe':
            for b in list(reads) + list(writes):
                if b.psum and b.xacc is not None and b.xacc[2] != e:
                    self._wait(e, b.xacc)
        ins = fn(self.eng[e])
        if sig:
            self.cnt[e] += 1
            ins.then_inc(self.sem[e], 1)
            tok = (self.sem[e], self.cnt[e], e)
        else:
            tok = (self.sem[e], self.cnt[e] + 1, e)
        self._update(tok, reads, writes)
        if e != 'pe':
            for b in list(reads) + list(writes):
                if b.psum:
                    b.xacc = tok

    def _get_dsem(self, owner, q):
        key = (id(owner), q)
        if key not in self.dsem:
            if self.free_sems[q]:
                sem = self.free_sems[q].pop()
            else:
                self.nsem += 1
                sem = self.es.enter_context(self.nc.semaphore('d%s%d' % (q, self.nsem)))
                self.dtot[id(sem)] = 0
                self.all_dsems.append(sem)
            self.dsem[key] = sem
        return self.dsem[key]

    def dma(self, q, out, in_, reads, writes, owner, **kw):
        self._deps(q, reads, writes, is_dma=True)
        sem = self._get_dsem(owner, q)
        self.dtot[id(sem)] += 16
        self.eng[q].dma_start(out=out, in_=in_, **kw).then_inc(sem, 16)
        tok = (sem, self.dtot[id(sem)], 'dma')
        self._update(tok, reads, writes)

    def barrier(self, release=True):
        for e in self.eng:
            for e2 in self.sem:
                if e2 != e and self.cnt[e2] > 0:
                    self._wait(e, (self.sem[e2], self.cnt[e2], e2))
            for sem in self.all_dsems:
                if self.dtot[id(sem)] > 0:
                    self._wait(e, (sem, self.dtot[id(sem)], 'dma'))
        if release:
            for (oid, q), sem in self.dsem.items():
                self.free_sems[q].append(sem)
            self.dsem = {}


def build_nc(NB, S, CTX):
    L = CTX + S
    NCX = CTX // 128
    NCL = S // 128
    NCH = NCX + NCL
    XOFF_C = 2
    XOFF_L = CTX + 6
    XW = CTX + S + 8
    nc = bass.Bass("TRN2", target_bir_lowering=False)

    def din(name, shape, dt=F32):
        return nc.dram_tensor(name, list(shape), dt, kind="ExternalInput").ap()

    def dscr(name, shape, dt):
        return nc.dram_tensor(name, list(shape), dt, kind="Internal").ap()

    x_d = din("x", [NB, S, D])
    ctx_d = din("ctx", [NB, CTX, D])
    cc_d = din("cc", [NB + 1, D])
    w_mod_d = din("w_mod", [D, 6 * D])
    b_mod_d = din("b_mod", [1, 6 * D])
    w_in_d = din("w_in", [D, INW])
    wkr_d = din("wkr", [D, 192])
    wq_d = din("wq", [384, NH * 96])
    wqs_d = din("wqs", [384, NH * 96])
    wk_d = din("wk", [256, NH * 64])
    wv_d = din("wv", [256, NH * 64])
    w_out_d = din("w_out", [2048, D])
    w_up_d = din("w_up", [D, 2 * DFF])
    w_down_d = din("w_down", [DFF, D])
    NV = 256
    vecs_d = din("vecs", [NV, 128])
    rows_d = din("rows", [4, D])
    small_d = din("small", [1, 96])
    ident_d = din("ident", [128, 128])
    uf_d = din("uf", [128, 128])
    ub_d = din("ub", [128, 128])
    c1_d = din("c1", [96, L])
    c2_d = din("c2", [96, L])
    out_d = nc.dram_tensor("out", [NB, S, D], F32, kind="ExternalOutput").ap()

    w_in_b = dscr("w_in_b", [D, INW], BF16)
    wkr_b = dscr("wkr_b", [D, 192], BF16)
    wq_b = dscr("wq_b", [384, NH * 96], BF16)
    wqs_b = dscr("wqs_b", [384, NH * 96], BF16)
    wk_b = dscr("wk_b", [256, NH * 64], BF16)
    wv_b = dscr("wv_b", [256, NH * 64], BF16)
    w_out_b = dscr("w_out_b", [2048, D], BF16)
    w_up_b = dscr("w_up_b", [D, 2 * DFF], BF16)
    w_down_b = dscr("w_down_b", [DFF, D], BF16)
    modv_d = dscr("modv", [NB + 1, 6 * D], F32)
    zs_d = dscr("zs", [S, D], BF16)
    x1_d = dscr("x1", [S, D], F32)
    attnT_d = dscr("attnT", [S // 128, 128, 8 * 128], BF16)
    ssdnT_d = dscr("ssdnT", [S // 128, 128, 8 * 128], BF16)
    sbst_d = dscr("sbst", [S // 128, 128, D], BF16)

    es = contextlib.ExitStack()
    with es:
        sc = Sched(nc, es)
        uid = [0]

        def sb(st, name, shape, dt):
            uid[0] += 1
            return Buf(st.enter_context(nc.sbuf_tensor("%s_%d" % (name, uid[0]), list(shape), dt)), name)

        def ps(st, name, shape, dt=F32):
            uid[0] += 1
            bf = Buf(st.enter_context(nc.psum_tensor("%s_%d" % (name, uid[0]), list(shape), dt)), name)
            bf.psum = True
            return bf

        def dbuf(ap, name):
            return Buf(ap, name)

        D_w = {n: dbuf(None, n) for n in ["w_in_b", "wkr_b", "wq_b", "wqs_b", "wk_b", "wv_b", "w_out_b", "w_up_b", "w_down_b"]}
        D_modv = dbuf(None, "modv")
        D_zs = dbuf(None, "zs")
        D_x1 = dbuf(None, "x1")
        D_attnT = dbuf(None, "attnT")
        D_ssdnT = dbuf(None, "ssdnT")
        D_out = dbuf(None, "out")

        def act(fn, reads, writes):
            sc.op('act', fn, reads, writes)

        def dve(fn, reads, writes):
            sc.op('dve', fn, reads, writes)

        def pool(fn, reads, writes):
            sc.op('pool', fn, reads, writes)

        def mm(out, lhsT, rhs, start, stop, reads, writes, sig=None):
            if sig is None:
                sig = stop
            sc.op('pe', lambda e: e.matmul(out, lhsT=lhsT, rhs=rhs, start=start, stop=stop), reads, writes, sig=sig)

        def tr(out, in_, idt, reads, writes, sig=True):
            sc.op('pe', lambda e: e.transpose(out, in_, idt), reads, writes, sig=sig)

        def rstd_from_ss(ssb, n):
            act(lambda e: e.activation(out=ssb[:, 1:2], in_=ssb[:, 0:1], func=AF.Ln, scale=1.0 / n, bias=epsb[:, 0:1]), [ssb, epsb], [ssb])
            act(lambda e: e.activation(out=ssb[:, 1:2], in_=ssb[:, 1:2], func=AF.Exp, scale=-0.5), [ssb], [ssb])

        gs = contextlib.ExitStack()
        es.enter_context(gs)
        identb = sb(gs, "identb", [128, 128], BF16)
        identf = sb(gs, "identf", [128, 128], F32)
        uf = sb(gs, "uf", [128, 128], F32)
        ub = sb(gs, "ub", [128, 128], F32)
        ufb = sb(gs, "ufb", [128, 128], BF16)
        ubb = sb(gs, "ubb", [128, 128], BF16)
        onesf = sb(gs, "onesf", [128, 128], F32)
        epsb = sb(gs, "epsb", [128, 1], F32)
        colv = sb(gs, "colv", [128, NV], F32)
        smallb = sb(gs, "smallb", [128, 96], F32)
        diagD = sb(gs, "diagD", [128, 8, 128], BF16)
        sc.dma('pool', identb[:], ident_d, [], [identb], identb)
        sc.dma('pool', ufb[:], uf_d, [], [ufb], ufb)
        sc.dma('pool', ubb[:], ub_d, [], [ubb], ubb)
        sc.dma('sp', identf[:], ident_d, [], [identf], identf)
        sc.dma('sp', uf[:], uf_d, [], [uf], uf)
        sc.dma('sp', ub[:], ub_d, [], [ub], ub)
        sc.dma('sp', smallb[:], small_d.partition_broadcast(128), [], [smallb], smallb)
        dve(lambda e: e.memset(onesf[:], 1.0), [], [onesf])
        dve(lambda e: e.memset(epsb[:], EPS), [], [epsb])
        act(lambda e: e.activation(out=smallb[:, 0:32], in_=smallb[:, 0:32], func=AF.Exp), [smallb], [smallb])
        dve(lambda e: e.tensor_scalar(out=smallb[:, 0:32], in0=smallb[:, 0:32], scalar1=-1.0, scalar2=None, op0=ALU.mult), [smallb], [smallb])

        R_MIXPRE, R_QN, R_KVN, R_CW, R_CB, R_SN, R_FPRE, R_FCW, R_FCB, R_DEXP = 0, 8, 11, 13, 73, 85, 93, 101, 167, 189
        with contextlib.ExitStack() as st:
            vt = sb(st, "vt", [128, 2, 128], F32)
            pv = ps(st, "pv", [128, 2, 128], F32)
            sc.dma('sp', vt[:], vecs_d.rearrange("(a p) c -> p a c", p=128), [], [vt], vt)
            for a in range(2):
                tr(pv[:, a, :], vt[:, a, :], identf[:], [vt, identf], [pv])
            dve(lambda e: e.tensor_copy(out=colv[:], in_=pv[:].rearrange("p a c -> p (a c)")), [pv], [colv])
            for j in range(8):
                dve(lambda e, j=j: e.tensor_scalar(out=diagD[:, j, :], in0=identf[:], scalar1=colv[:, R_DEXP + j:R_DEXP + j + 1], scalar2=None, op0=ALU.mult),
                    [identf, colv], [diagD])
            sc.barrier()

        castlist = [(w_in_b, w_in_d, "w_in_b"), (wkr_b, wkr_d, "wkr_b"), (wq_b, wq_d, "wq_b"), (wqs_b, wqs_d, "wqs_b"),
                    (wk_b, wk_d, "wk_b"), (wv_b, wv_d, "wv_b"), (w_out_b, w_out_d, "w_out_b"), (w_up_b, w_up_d, "w_up_b"),
                    (w_down_b, w_down_d, "w_down_b")]
        def do_casts(lst):
            for dst, src, nm in lst:
                rows = dst.shape[0]
                step = 256
                for r0 in range(0, rows, step):
                    r1 = min(rows, r0 + step)
                    sc.dma('pool', dst[r0:r1, :], src[r0:r1, :], [], [D_w[nm]], D_w[nm])

        do_casts(castlist[0:2])

        NR = NB + 1
        with contextlib.ExitStack() as st:
            ccs = sb(st, "ccs", [NR, D], F32)
            sig = sb(st, "sig", [NR, D], F32)
            scT = sb(st, "scT", [128, 8, NR], F32)
            bmod = sb(st, "bmod", [NR, 6 * D], F32)
            modsb = sb(st, "modsb", [NR, 6 * D], F32)
            wm = [sb(st, "wm%d" % i, [128, 8, 512], F32) for i in range(2)]
            pT0 = ps(st, "pT0", [128, 8, NR], F32)
            pm = [ps(st, "pm%d" % i, [NR, 512], F32) for i in range(2)]
            sc.dma('sp', ccs[:], cc_d, [], [ccs], ccs)
            sc.dma('sp', bmod[:], b_mod_d.partition_broadcast(NR), [], [bmod], bmod)
            act(lambda e: e.activation(out=sig[:], in_=ccs[:], func=AF.Exp, scale=-1.0), [ccs], [sig])
            dve(lambda e: e.tensor_scalar(out=sig[:], in0=sig[:], scalar1=1.0, scalar2=None, op0=ALU.add), [sig], [sig])
            dve(lambda e: e.reciprocal(out=sig[:], in_=sig[:]), [sig], [sig])
            dve(lambda e: e.tensor_tensor(out=ccs[:], in0=ccs[:], in1=sig[:], op=ALU.mult), [ccs, sig], [ccs])
            for k in range(8):
                tr(pT0[:, k, :], ccs[:, k * 128:(k + 1) * 128], identf[0:NR, 0:NR], [ccs, identf], [pT0])
            dve(lambda e: e.tensor_copy(out=scT[:], in_=pT0[:]), [pT0], [scT])
            for n in range(12):
                w = wm[n % 2]
                sc.dma('sp', w[:], w_mod_d[:, n * 512:(n + 1) * 512].rearrange("(k p) c -> p k c", p=128), [], [w], w)
                p = pm[n % 2]
                for k in range(8):
                    mm(p[:], scT[:, k, :], w[:, k, :], k == 0, k == 7, [scT, w], [p])
                dve(lambda e, p=p, n=n: e.tensor_tensor(out=modsb[:, n * 512:(n + 1) * 512], in0=p[:], in1=bmod[:, n * 512:(n + 1) * 512], op=ALU.add),
                    [p, bmod], [modsb])
            sc.dma('sp', modv_d, modsb[:], [modsb], [D_modv], modsb)
            sc.barrier()

        for b in range(NB):
            ms = contextlib.ExitStack()
            es.enter_context(ms)
            modT = [sb(ms, "modT%d" % i, [128, 48], F32) for i in range(2)]
            G1 = [sb(ms, "G1_%d" % i, [128, 8], F32) for i in range(2)]
            G4 = sb(ms, "G4", [128, 8], F32)
            bs = contextlib.ExitStack()
            es.enter_context(bs)
            cqnT = sb(bs, "cqnT", [128, 3, S], BF16)
            ckvnT = sb(bs, "ckvnT", [128, 2, L], BF16)
            krT = sb(bs, "krT", [96, L], BF16)
            dtv = sb(bs, "dtv", [128, NCH, 32], F32)
            xstk = contextlib.ExitStack()
            es.enter_context(xstk)
            xbcT = [sb(xstk, "xbcT%d" % j, [128, XW], BF16) for j in range(12)]

            with contextlib.ExitStack() as st:
                mt = sb(st, "mt", [48, 2, 128], F32)
                pmt = ps(st, "pmt", [128, 2, 48], F32)
                for i, r in enumerate([b, NB]):
                    sc.dma('sp', mt[:, i, :], modv_d[r:r + 1, :].rearrange("o (c p) -> (o c) p", p=128), [D_modv], [mt], mt)
                for i in range(2):
                    tr(pmt[:, i, :], mt[:, i, :], identf[0:48, 0:48], [mt, identf], [pmt])
                    dve(lambda e, i=i: e.tensor_copy(out=modT[i][:], in_=pmt[:, i, :]), [pmt], [modT[i]])
                    dve(lambda e, i=i: e.scalar_tensor_tensor(out=G1[i][:], in0=modT[i][:, 8:16], scalar=1.0, in1=colv[:, R_MIXPRE:R_MIXPRE + 8], op0=ALU.add, op1=ALU.mult),
                        [modT[i], colv], [G1[i]])
                dve(lambda e: e.scalar_tensor_tensor(out=G4[:], in0=modT[0][:, 32:40], scalar=1.0, in1=colv[:, R_FPRE:R_FPRE + 8], op0=ALU.add, op1=ALU.mult),
                    [modT[0], colv], [G4])
                sc.barrier()

            if b == 0:
                do_casts(castlist[2:])
            with contextlib.ExitStack() as st:
                w_in_sb = sb(st, "w_in_sb", [128, 8, INW], BF16)
                wkr_sb = sb(st, "wkr_sb", [128, 8, 192], BF16)
                c1 = sb(st, "c1", [96, L], BF16)
                c2 = sb(st, "c2", [96, L], BF16)
                dtb = smallb
                xt = [sb(st, "xt%d" % i, [128, D], F32) for i in range(2)]
                xs = [sb(st, "xs%d" % i, [128, D], BF16) for i in range(2)]
                junk = sb(st, "junk", [128, D], BF16)
                junk2 = sb(st, "junk2", [128, 640], BF16)
                ssb = [sb(st, "ssb%d" % i, [128, 2], F32) for i in range(2)]
                ssq = [sb(st, "ssq%d" % i, [128, 2], F32) for i in range(2)]
                sskv = [sb(st, "sskv%d" % i, [128, 2], F32) for i in range(2)]
                hT = [sb(st, "hT%d" % i, [128, 8, 512], BF16) for i in range(2)]
                hTk = [[Buf(hT[i].t, "hT%d_%d" % (i, k)) for k in range(8)] for i in range(2)]
                qkv = [sb(st, "qkv%d" % i, [128, 640], BF16) for i in range(2)]
                zsb = [sb(st, "zsb%d" % i, [128, D], BF16) for i in range(2)]
                dtt = sb(st, "dtt", [128, 32], F32)
                pAs = [sb(st, "pAs%d" % i, [128, 672], F32) for i in range(2)]
                tmpT = [sb(st, "tmpT%d" % i, [128, 8, 128], BF16) for i in range(2)]
                kra = sb(st, "kra", [96, 512], F32)
                krb = sb(st, "krb", [96, 512], F32)
                pT = ps(st, "pT", [128, 8, 128], BF16)
                pT2 = ps(st, "pT2", [128, 8, 128], BF16)
                pA = ps(st, "pA", [128, 1024], F32)
                pZ = ps(st, "pZ", [128, 1024], F32)
                pF = [ps(st, "pF%d" % i, [128, 512], F32) for i in range(2)]
                cqk = [Buf(cqnT.t, "cqnT%d" % k) for k in range(3)]
                ckvk = [Buf(ckvnT.t, "ckvnT%d" % k) for k in range(2)]
                sc.dma('sp', w_in_sb[:], w_in_b.rearrange("(k p) c -> p k c", p=128), [D_w["w_in_b"]], [w_in_sb], w_in_sb)
                sc.dma('sp', wkr_sb[:], wkr_b.rearrange("(k p) c -> p k c", p=128), [D_w["wkr_b"]], [wkr_sb], wkr_sb)
                sc.dma('pool', c1[:], c1_d, [], [c1], c1)
                sc.dma('pool', c2[:], c2_d, [], [c2], c2)
                for j in range(12):
                    pool(lambda e, j=j: e.memset(xbcT[j][:], 0.0), [], [xbcT[j]])
                blocks = [(0, CTX, True)] + [(CTX + i * 512, 512, False) for i in range(S // 512)]
                tiles = []
                for bi, (t0, T, isctx) in enumerate(blocks):
                    for tt in range(T // 128):
                        tiles.append((bi, t0, T, isctx, tt, tt == T // 128 - 1))
                fcount = [0]

                def p1_front(ti):
                    bi, t0, T, isctx, tt, lastt = tiles[ti]
                    h = hT[bi % 2]; hk = hTk[bi % 2]
                    mi = 1 if isctx else 0
                    g0 = t0 + tt * 128
                    x_ = xt[ti % 2]; xs_ = xs[ti % 2]; ss_ = ssb[ti % 2]
                    src = ctx_d[b, g0:g0 + 128, :] if isctx else x_d[b, g0 - CTX:g0 - CTX + 128, :]
                    sc.dma('sp', x_[:], src, [], [x_], x_)
                    act(lambda e: e.activation(out=junk[:], in_=x_[:], func=AF.Square, accum_out=ss_[:, 0:1]), [x_], [junk, ss_])
                    rstd_from_ss(ss_, D)
                    dve(lambda e: e.tensor_scalar(out=xs_[:], in0=x_[:], scalar1=ss_[:, 1:2], scalar2=None, op0=ALU.mult), [x_, ss_], [xs_])
                    for k in range(8):
                        tr(pT[:, k, :], xs_[:, k * 128:(k + 1) * 128], identb[:], [xs_, identb], [pT], sig=(k == 7))
                    tm_ = tmpT[ti % 2]
                    dve(lambda e: e.tensor_tensor(out=tm_[:], in0=pT[:], in1=G1[mi][:].unsqueeze(2).to_broadcast([128, 8, 128]), op=ALU.mult), [pT, G1[mi]], [tm_])
                    dve(lambda e: e.tensor_tensor(out=h[:, :, tt * 128:(tt + 1) * 128], in0=tm_[:], in1=modT[mi][:, 0:8].unsqueeze(2).to_broadcast([128, 8, 128]), op=ALU.add),
                        [tm_, modT[mi]], hk)

                def p1_back(ti):
                    bi, t0, T, isctx, tt, lastt = tiles[ti]
                    h = hT[bi % 2]; hk = hTk[bi % 2]
                    g0 = t0 + tt * 128
                    ch = g0 // 128
                    hs = lambda k: h[:, k, tt * 128:(tt + 1) * 128]
                    for (c0, c1_, dst0) in [(0, 512, 0), (512, 640, 512)]:
                        for k in range(8):
                            mm(pA[:, dst0:dst0 + (c1_ - c0)], hs(k), w_in_sb[:, k, c0:c1_], k == 0, k == 7, [hk[k], w_in_sb], [pA])
                    for k in range(8):
                        mm(pA[:, 640:672], hs(k), w_in_sb[:, k, 3232:3264], k == 0, k == 7, [hk[k], w_in_sb], [pA])
                    pAs_ = pAs[ti % 2]
                    act(lambda e: e.activation(out=pAs_[:], in_=pA[:, 0:672], func=AF.Copy), [pA], [pAs_])
                    if not isctx:
                        for (c0, dst0) in [(672, 0), (1184, 512)]:
                            for k in range(8):
                                mm(pZ[:, dst0:dst0 + 512], hs(k), w_in_sb[:, k, c0:c0 + 512], k == 0, k == 7, [hk[k], w_in_sb], [pZ])
                        z_ = zsb[ti % 2]
                        act(lambda e: e.activation(out=z_[:], in_=pZ[:], func=AF.Silu), [pZ], [z_])
                        sc.dma('sp', zs_d[g0 - CTX:g0 - CTX + 128, :], z_[:], [z_], [D_zs], z_)
                    if lastt:
                        p1_blockmm(ti)

                def p1_back_b(ti):
                    bi, t0, T, isctx, tt, lastt = tiles[ti]
                    g0 = t0 + tt * 128
                    ch = g0 // 128
                    pAs_ = pAs[ti % 2]
                    dve(lambda e: e.tensor_tensor(out=dtt[:], in0=pAs_[:, 640:672], in1=dtb[:, 32:64], op=ALU.add), [pAs_, dtb], [dtt])
                    act(lambda e: e.activation(out=dtt[:], in_=dtt[:], func=AF.Exp), [dtt], [dtt])
                    act(lambda e: e.activation(out=dtv[:, ch, :], in_=dtt[:], func=AF.Ln, bias=1.0), [dtt], [dtv])
                    q_ = qkv[ti % 2]; sq_ = ssq[ti % 2]; skv_ = sskv[ti % 2]
                    act(lambda e: e.activation(out=junk2[:, 0:384], in_=pAs_[:, 0:384], func=AF.Square, accum_out=sq_[:, 0:1]), [pAs_], [junk2, sq_])
                    act(lambda e: e.activation(out=junk2[:, 384:640], in_=pAs_[:, 384:640], func=AF.Square, accum_out=skv_[:, 0:1]), [pAs_], [junk2, skv_])
                    rstd_from_ss(sq_, 384)
                    rstd_from_ss(skv_, 256)
                    dve(lambda e: e.tensor_scalar(out=q_[:, 0:384], in0=pAs_[:, 0:384], scalar1=sq_[:, 1:2], scalar2=None, op0=ALU.mult), [pAs_, sq_], [q_])
                    dve(lambda e: e.tensor_scalar(out=q_[:, 384:640], in0=pAs_[:, 384:640], scalar1=skv_[:, 1:2], scalar2=None, op0=ALU.mult), [pAs_, skv_], [q_])

                def p1_back_tr(ti):
                    bi, t0, T, isctx, tt, lastt = tiles[ti]
                    g0 = t0 + tt * 128
                    q_ = qkv[ti % 2]
                    for k in range(5):
                        tr(pT2[:, k, :], q_[:, k * 128:(k + 1) * 128], identb[:], [q_, identb], [pT2], sig=(k == 4))
                    if not isctx:
                        dve(lambda e: e.tensor_tensor(out=cqnT[:, :, g0 - CTX:g0 - CTX + 128], in0=pT2[:, 0:3, :],
                                                      in1=colv[:, R_QN:R_QN + 3].unsqueeze(2).to_broadcast([128, 3, 128]), op=ALU.mult), [pT2, colv], cqk)
                    dve(lambda e: e.tensor_tensor(out=ckvnT[:, :, g0:g0 + 128], in0=pT2[:, 3:5, :],
                                                  in1=colv[:, R_KVN:R_KVN + 2].unsqueeze(2).to_broadcast([128, 2, 128]), op=ALU.mult), [pT2, colv], ckvk)

                def p1_blockmm(ti):
                    bi, t0, T, isctx, tt, lastt = tiles[ti]
                    h = hT[bi % 2]; hk = hTk[bi % 2]
                    if True:
                        xoff = (XOFF_C + t0) if isctx else (XOFF_L + t0 - CTX)
                        for j in range(12):
                            p = pF[fcount[0] % 2]; fcount[0] += 1
                            for k in range(8):
                                mm(p[:, 0:T], w_in_sb[:, k, 1696 + j * 128:1696 + (j + 1) * 128], h[:, k, 0:T], k == 0, k == 7, [hk[k], w_in_sb], [p])
                            if j % 2 == 0:
                                act(lambda e, p=p, j=j: e.activation(out=xbcT[j][:, xoff:xoff + T], in_=p[:, 0:T], func=AF.Copy), [p], [xbcT[j]])
                            else:
                                dve(lambda e, p=p, j=j: e.tensor_copy(out=xbcT[j][:, xoff:xoff + T], in_=p[:, 0:T]), [p], [xbcT[j]])
                        pa = pF[fcount[0] % 2]; fcount[0] += 1
                        pb = pF[fcount[0] % 2]; fcount[0] += 1
                        for k in range(8):
                            mm(pa[0:96, 0:T], wkr_sb[:, k, 0:96], h[:, k, 0:T], k == 0, k == 7, [hk[k], wkr_sb], [pa])
                        for k in range(8):
                            mm(pb[0:96, 0:T], wkr_sb[:, k, 96:192], h[:, k, 0:T], k == 0, k == 7, [hk[k], wkr_sb], [pb])
                        dve(lambda e: e.tensor_tensor(out=kra[:, 0:T], in0=pa[0:96, 0:T], in1=c1[:, t0:t0 + T], op=ALU.mult), [pa, c1], [kra])
                        dve(lambda e: e.tensor_tensor(out=krb[:, 0:T], in0=pb[0:96, 0:T], in1=c2[:, t0:t0 + T], op=ALU.mult), [pb, c2], [krb])
                        pool(lambda e: e.tensor_tensor(out=krT[:, t0:t0 + T], in0=kra[:, 0:T], in1=krb[:, 0:T], op=ALU.add), [kra, krb], [krT])

                p1_front(0)
                nt_ = len(tiles)
                for ti in range(nt_):
                    if ti + 1 < nt_:
                        p1_front(ti + 1)
                    p1_back(ti)
                    if ti >= 1:
                        p1_back_b(ti - 1)
                    if ti >= 2:
                        p1_back_tr(ti - 2)
                p1_back_b(nt_ - 1)
                if nt_ >= 2:
                    p1_back_tr(nt_ - 2)
                p1_back_tr(nt_ - 1)
                sc.barrier()

            with contextlib.ExitStack() as st:
                dW = sb(st, "dW", [128, 60, 128], BF16)
                cto = [sb(st, "cto%d" % i, [128, L], BF16) for i in range(2)]
                pcv = [ps(st, "pcv%d" % i, [128, 512], F32) for i in range(4)]
                for kk in range(5):
                    for j in range(12):
                        r = kk * 12 + j
                        dve(lambda e, r=r: e.tensor_scalar(out=dW[:, r, :], in0=identf[:], scalar1=colv[:, R_CW + r:R_CW + r + 1], scalar2=None, op0=ALU.mult), [identf, colv], [dW])
                pi = 0
                cblocks = [(XOFF_C, 0, CTX)] + [(XOFF_L + i * 512, CTX + i * 512, 512) for i in range(S // 512)]
                for j in range(12):
                    t_ = cto[j % 2]
                    xb_ = xbcT[j]
                    for (off, o0, n) in cblocks:
                        p = pcv[pi % 4]; pi += 1
                        for kk in range(5):
                            mm(p[:, 0:n], dW[:, kk * 12 + j, :], xb_[:, off + kk - 2:off + kk - 2 + n], kk == 0, kk == 4, [dW, xb_], [p])
                        act(lambda e, p=p, o0=o0, n=n, j=j: e.activation(out=t_[:, o0:o0 + n], in_=p[:, 0:n], func=AF.Silu, bias=colv[:, R_CB + j:R_CB + j + 1]), [p, colv], [t_])
                    dve(lambda e, t_=t_, xb_=xb_: e.tensor_copy(out=xb_[:, XOFF_C:XOFF_C + CTX], in_=t_[:, 0:CTX]), [t_], [xb_])
                    dve(lambda e, t_=t_, xb_=xb_: e.tensor_copy(out=xb_[:, XOFF_L:XOFF_L + S], in_=t_[:, CTX:L]), [t_], [xb_])
                sc.barrier()

            def xcol(ch):
                return XOFF_C + ch * 128 if ch < NCX else XOFF_L + (ch - NCX) * 128

            with contextlib.ExitStack() as st:
                NS = NCH * 32
                a_all = sb(st, "a_all", [128, NCH, 32], F32)
                acs_all = sb(st, "acs_all", [128, NCH, 32], F32)
                coef_all = sb(st, "coef_all", [128, NCH, 32], F32)
                dec_all = sb(st, "dec_all", [128, NCH, 32], F32)
                ea_all = sb(st, "ea_all", [128, NCH, 32], F32)
                Sf = sb(st, "Sf", [128, D], F32)
                Sbk = sb(st, "Sbk", [128, D], F32)
                Sfb = sb(st, "Sfb", [128, D], BF16)
                Sbt = [sb(st, "Sbt%d" % i, [128, D], BF16) for i in range(2)]
                Sst = [sb(st, "Sst%d" % i, [128, D], BF16) for i in range(2)]
                xtm = [sb(st, "xtm%d" % i, [128, D], BF16) for i in range(2)]
                btm = [sb(st, "btm%d" % i, [128, 256], BF16) for i in range(2)]
                xd = [[sb(st, "xd%d_%d" % (d, i), [128, D], BF16) for i in range(2)] for d in range(2)]
                xdw = [sb(st, "xdw%d" % i, [128, D], BF16) for i in range(2)]
                AUh = [sb(st, "AUh%d" % i, [128, 16, 128], BF16) for i in range(2)]
                AUl = [sb(st, "AUl%d" % i, [128, 16, 128], BF16) for i in range(2)]
                a_hi = sb(st, "a_hi", [128, NCH, 32], BF16)
                a_lo = sb(st, "a_lo", [128, NCH, 32], BF16)
                negU = [sb(st, "negU%d" % d, [128, 128], BF16) for d in range(2)]
                onesb = sb(st, "onesb", [128, 128], BF16)
                Eb = [sb(st, "Eb%d" % d, [128, 16, 128], BF16) for d in range(2)]
                Gs = [sb(st, "Gs%d" % i, [128, 2, 128], BF16) for i in range(2)]
                MT = [[sb(st, "MT%d_%d" % (d, i), [128, 16, 128], BF16) for i in range(2)] for d in range(2)]
                t1 = sb(st, "t1", [128, D], F32)
                yg = sb(st, "yg", [128, D], F32)
                ygb = sb(st, "ygb", [128, D], BF16)
                junk3 = sb(st, "junk3", [128, D], BF16)
                zt = [sb(st, "zt%d" % i, [128, D], BF16) for i in range(2)]
                snT = [sb(st, "snT%d" % i, [128, 8, 128], BF16) for i in range(2)]
                ss3 = [sb(st, "ss3_%d" % i, [128, 2], F32) for i in range(2)]
                negb = [sb(st, "negb%d" % d, [128, 128], BF16) for d in range(2)]
                pX = ps(st, "pX", [128, 1024], BF16)
                pRb = [ps(st, "pRb%d" % i, [128, 512], F32) for i in range(2)]
                pY = ps(st, "pY", [128, 1024], F32)
                pO = [ps(st, "pO3_%d" % i, [128, 512], F32) for i in range(3)]
                mask = {0: uf, 1: ub}
                ocnt = [0]
                rcnt = [0]

                def nextO():
                    ocnt[0] += 1
                    return pO[ocnt[0] % 3]

                for d in range(2):
                    dve(lambda e, d=d: e.tensor_scalar(out=negb[d][:], in0=mask[d][:], scalar1=-1.0, scalar2=30000.0, op0=ALU.add, op1=ALU.mult), [mask[d]], [negb[d]])
                dve(lambda e: e.tensor_tensor(out=a_all[:], in0=dtv[:], in1=smallb[:, 0:32].unsqueeze(1).to_broadcast([128, NCH, 32]), op=ALU.mult), [dtv, smallb], [a_all])
                for d in range(2):
                    dsl = slice(16 * d, 16 * d + 16)
                    p1_ = nextO(); p2_ = nextO()
                    mm(p1_[:, 0:NCH * 16], mask[d][:], a_all[:, :, dsl], True, True, [mask[d], a_all], [p1_])
                    mm(p2_[:, 0:NCH * 16], onesf[:], a_all[:, :, dsl], True, True, [onesf, a_all], [p2_])
                    dve(lambda e, p1_=p1_, dsl=dsl: e.tensor_copy(out=acs_all[:, :, dsl], in_=p1_[:, 0:NCH * 16].rearrange("p (c h) -> p c h", h=16)), [p1_], [acs_all])
                    dve(lambda e, p2_=p2_, dsl=dsl: e.tensor_tensor(out=coef_all[:, :, dsl], in0=p2_[:, 0:NCH * 16].rearrange("p (c h) -> p c h", h=16), in1=acs_all[:, :, dsl], op=ALU.subtract),
                        [p2_, acs_all], [coef_all])
                    act(lambda e, p2_=p2_, dsl=dsl: e.activation(out=dec_all[:, :, dsl], in_=p2_[:, 0:NCH * 16].rearrange("p (c h) -> p c h", h=16), func=AF.Exp), [p2_], [dec_all])
                act(lambda e: e.activation(out=coef_all[:], in_=coef_all[:], func=AF.Exp), [coef_all], [coef_all])
                act(lambda e: e.activation(out=ea_all[:], in_=acs_all[:], func=AF.Exp), [acs_all], [ea_all])
                dve(lambda e: e.tensor_copy(out=a_hi[:], in_=a_all[:]), [a_all], [a_hi])
                dve(lambda e: e.tensor_tensor(out=a_lo[:], in0=a_all[:], in1=a_hi[:], op=ALU.subtract), [a_all, a_hi], [a_lo])
                dve(lambda e: e.memset(onesb[:], 1.0), [], [onesb])
                for d in range(2):
                    dve(lambda e, d=d: e.tensor_scalar(out=negU[d][:], in0=mask[d][:], scalar1=-1.0, scalar2=None, op0=ALU.mult), [mask[d]], [negU[d]])
                dve(lambda e: e.tensor_tensor(out=coef_all[:], in0=coef_all[:], in1=dtv[:], op=ALU.mult), [coef_all, dtv], [coef_all])

                def h3(ap_):
                    return ap_.rearrange("p (h q) -> p h q", h=16)

                def bc64(ap_):
                    return ap_.unsqueeze(2).to_broadcast([128, 16, 64])

                def tokmajor(ch, slot):
                    c0 = xcol(ch)
                    x_ = xtm[slot]; b_ = btm[slot]
                    for j in range(8):
                        tr(pX[:, j * 128:(j + 1) * 128], xbcT[j][:, c0:c0 + 128], identb[:], [xbcT[j], identb], [pX], sig=(j == 7))
                    act(lambda e: e.activation(out=x_[:], in_=pX[:], func=AF.Copy), [pX], [x_])
                    for j in range(2):
                        tr(pX[:, j * 128:(j + 1) * 128], xbcT[8 + j][:, c0:c0 + 128], identb[:], [xbcT[8 + j], identb], [pX], sig=(j == 1))
                    act(lambda e: e.activation(out=b_[:], in_=pX[:, 0:256], func=AF.Copy), [pX], [b_])

                bkc = [0]
                pBk = pO + pRb

                def nextBk():
                    bkc[0] += 1
                    return pBk[bkc[0] % 5]

                def state_A(ch, d, slot, nextO=nextO):
                    tokmajor(ch, slot)
                    x_ = xtm[slot]; b_ = btm[slot]; w_ = xdw[slot]
                    pool(lambda e: e.tensor_tensor(out=h3(w_[:]), in0=h3(x_[:]), in1=bc64(coef_all[:, ch, 16 * d:16 * d + 16]), op=ALU.mult), [x_, coef_all], [w_])
                    banks = []
                    for g in range(2):
                        p = nextO()
                        mm(p[:], b_[:, g * 128:(g + 1) * 128], w_[:, g * 512:(g + 1) * 512], True, True, [b_, w_], [p])
                        banks.append(p)
                    return banks

                def state_B(ch, d, Sst_, banks):
                    dve(lambda e: e.tensor_tensor(out=h3(Sst_[:]), in0=h3(Sst_[:]), in1=bc64(dec_all[:, ch, 16 * d:16 * d + 16]), op=ALU.mult), [Sst_, dec_all], [Sst_])
                    for g in range(2):
                        dve(lambda e, g=g: e.tensor_tensor(out=Sst_[:, g * 512:(g + 1) * 512], in0=Sst_[:, g * 512:(g + 1) * 512], in1=banks[g][:], op=ALU.add), [Sst_, banks[g]], [Sst_])

                D_sb = dbuf(None, "sbst")
                skc = [0]

                def store_Sb(lc):
                    s_ = Sst[skc[0] % 2]; skc[0] += 1
                    act(lambda e: e.activation(out=s_[:], in_=Sbk[:], func=AF.Copy), [Sbk], [s_])
                    sc.dma('sp', sbst_d[lc], s_[:], [s_], [D_sb], s_)

                dve(lambda e: e.memset(Sbk[:], 0.0), [], [Sbk])
                dve(lambda e: e.memset(Sf[:], 0.0), [], [Sf])
                order_b = list(range(NCX - 1, -1, -1)) + list(range(NCH - 1, NCX, -1))
                prev = None
                for i, ch in enumerate(order_b):
                    banks = state_A(ch, 1, i % 2, nextBk)
                    if prev is not None:
                        pch, pbanks = prev
                        if pch >= NCX:
                            store_Sb(pch - NCX)
                        state_B(pch, 1, Sbk, pbanks)
                    prev = (ch, banks)
                pch, pbanks = prev
                if pch >= NCX:
                    store_Sb(pch - NCX)
                state_B(pch, 1, Sbk, pbanks)
                store_Sb(0)
                sc.barrier(release=False)
                for i, ch in enumerate(range(NCX)):
                    banks = state_A(ch, 0, i % 2)
                    state_B(ch, 0, Sf, banks)

                maskbf = {0: ufb, 1: ubb}

                def lat_AU(ch, ds=(0, 1)):
                    for d in ds:
                        dve(lambda e, d=d: e.tensor_tensor(out=AUh[d][:], in0=maskbf[d][:].unsqueeze(1).to_broadcast([128, 16, 128]),
                                                           in1=a_hi[:, ch, 16 * d:16 * d + 16].unsqueeze(2).to_broadcast([128, 16, 128]), op=ALU.mult), [maskbf[d], a_hi], [AUh[d]])
                        dve(lambda e, d=d: e.tensor_tensor(out=AUl[d][:], in0=maskbf[d][:].unsqueeze(1).to_broadcast([128, 16, 128]),
                                                           in1=a_lo[:, ch, 16 * d:16 * d + 16].unsqueeze(2).to_broadcast([128, 16, 128]), op=ALU.mult), [maskbf[d], a_lo], [AUl[d]])

                def lat_A(ch, slot):
                    lc = ch - NCX
                    c0 = xcol(ch)
                    x_ = xtm[slot]; w_ = xdw[slot]
                    tokmajor(ch, slot)
                    sc.dma('sp', zt[slot][:], zs_d[lc * 128:(lc + 1) * 128, :], [D_zs], [zt[slot]], zt[slot])
                    sc.dma('sp', Sbt[slot][:], sbst_d[lc], [D_sb], [Sbt[slot]], Sbt[slot])
                    for d in range(2):
                        pool(lambda e, d=d: e.tensor_tensor(out=h3(xd[d][slot][:]), in0=h3(x_[:]), in1=bc64(dtv[:, ch, 16 * d:16 * d + 16]), op=ALU.mult), [x_, dtv], [xd[d][slot]])
                    pool(lambda e: e.tensor_tensor(out=h3(w_[:]), in0=h3(x_[:]), in1=bc64(coef_all[:, ch, 0:16]), op=ALU.mult), [x_, coef_all], [w_])
                    pg = nextO()
                    for g in range(2):
                        mm(pg[:, g * 128:(g + 1) * 128], xbcT[8 + g][:, c0:c0 + 128], xbcT[10 + g][:, c0:c0 + 128], True, True, [xbcT[8 + g], xbcT[10 + g]], [pg], sig=(g == 1))
                    act(lambda e: e.activation(out=Gs[slot][:], in_=pg[:, 0:256].rearrange("p (g l) -> p g l", g=2), func=AF.Copy), [pg], [Gs[slot]])
                    for d in range(2):
                        for r4 in range(4):
                            pr = pRb[rcnt[0] % 2]; rcnt[0] += 1
                            hsl = slice(16 * d + r4 * 4, 16 * d + r4 * 4 + 4)
                            mm(pr[:], onesb[:], AUh[d][:, r4 * 4:(r4 + 1) * 4, :].rearrange("p h l -> p (h l)"), True, False, [onesb, AUh[d]], [pr], sig=False)
                            mm(pr[:], onesb[:], AUl[d][:, r4 * 4:(r4 + 1) * 4, :].rearrange("p h l -> p (h l)"), False, False, [onesb, AUl[d]], [pr], sig=False)
                            mm(pr[:], negU[d][:], a_hi[:, ch, hsl].unsqueeze(2).to_broadcast([128, 4, 128]), False, False, [negU[d], a_hi], [pr], sig=False)
                            mm(pr[:], negU[d][:], a_lo[:, ch, hsl].unsqueeze(2).to_broadcast([128, 4, 128]), False, False, [negU[d], a_lo], [pr], sig=False)
                            mm(pr[:], identb[:], negb[d][:].unsqueeze(1).to_broadcast([128, 4, 128]), False, True, [identb, negb[d]], [pr], sig=True)
                            act(lambda e, d=d, r4=r4, pr=pr: e.activation(out=Eb[d][:, r4 * 4:(r4 + 1) * 4, :], in_=pr[:].rearrange("p (h l) -> p h l", h=4), func=AF.Exp), [pr], [Eb[d]])
                        en = 'dve'
                        for g in range(2):
                            sc.op(en, lambda e, d=d, g=g: e.tensor_tensor(out=MT[d][slot][:, g * 8:(g + 1) * 8, :], in0=Eb[d][:, g * 8:(g + 1) * 8, :],
                                                                        in1=Gs[slot][:, g:g + 1, :].to_broadcast([128, 8, 128]), op=ALU.mult), [Eb[d], Gs[slot]], [MT[d][slot]])

                def lat_B(ch, slot):
                    lc = ch - NCX
                    c0 = xcol(ch)
                    b_ = btm[slot]; w_ = xdw[slot]; s3 = ss3[slot]
                    act(lambda e: e.activation(out=Sfb[:], in_=Sf[:], func=AF.Copy), [Sf], [Sfb])
                    for j in range(8):
                        mm(pY[:, j * 128:(j + 1) * 128], xbcT[j][:, c0:c0 + 128], diagD[:, j, :], j % 4 == 0, False, [xbcT[j], diagD], [pY], sig=False)
                    for d in range(2):
                        for hh in range(16):
                            last = (d == 1)
                            mm(pY[:, hh * 64:(hh + 1) * 64], MT[d][slot][:, hh, :], xd[d][slot][:, hh * 64:(hh + 1) * 64], False, last and hh % 8 == 7,
                               [MT[d][slot], xd[d][slot]], [pY], sig=(last and hh == 15))
                    for d in range(2):
                        srcS = Sfb if d == 0 else Sbt[slot]
                        dst = t1 if d == 0 else yg
                        for g in range(2):
                            p = nextO()
                            mm(p[:], xbcT[10 + g][:, c0:c0 + 128], srcS[:, g * 512:(g + 1) * 512], True, True, [xbcT[10 + g], srcS], [p])
                            dve(lambda e, p=p, g=g, d=d, dst=dst: e.tensor_tensor(out=dst[:, g * 512:(g + 1) * 512].rearrange("p (h q) -> p h q", h=8),
                                                                                 in0=p[:].rearrange("p (h q) -> p h q", h=8),
                                                                                 in1=ea_all[:, ch, 16 * d + 8 * g:16 * d + 8 * g + 8].unsqueeze(2).to_broadcast([128, 8, 64]), op=ALU.mult),
                                [p, ea_all], [dst])
                    if ch < NCH - 1:
                        banks = []
                        for g in range(2):
                            p = nextO()
                            mm(p[:], b_[:, g * 128:(g + 1) * 128], w_[:, g * 512:(g + 1) * 512], True, True, [b_, w_], [p])
                            banks.append(p)
                        state_B(ch, 0, Sf, banks)

                def lat_B1b(ch, slot):
                    s3 = ss3[slot]
                    pool(lambda e: e.tensor_tensor(out=t1[:], in0=t1[:], in1=yg[:], op=ALU.add), [t1, yg], [t1])
                    dve(lambda e: e.tensor_tensor(out=yg[:], in0=pY[:], in1=t1[:], op=ALU.add), [pY, t1], [yg])

                def lat_B1c(ch, slot):
                    s3 = ss3[slot]
                    pool(lambda e: e.tensor_tensor(out=yg[:], in0=yg[:], in1=zt[slot][:], op=ALU.mult), [yg, zt[slot]], [yg])
                    act(lambda e: e.activation(out=junk3[:], in_=yg[:], func=AF.Square, accum_out=s3[:, 0:1]), [yg], [junk3, s3])
                    rstd_from_ss(s3, D)
                    act(lambda e: e.activation(out=ygb[:], in_=yg[:], func=AF.Copy, scale=s3[:, 1:2]), [yg, s3], [ygb])

                def lat_B2(ch, slot):
                    lc = ch - NCX
                    for j in range(8):
                        tr(pX[:, j * 128:(j + 1) * 128], ygb[:, j * 128:(j + 1) * 128], identb[:], [ygb, identb], [pX], sig=(j == 7))
                    dve(lambda e: e.tensor_tensor(out=snT[slot][:], in0=pX[:].rearrange("p (j t) -> p j t", j=8),
                                                  in1=colv[:, R_SN:R_SN + 8].unsqueeze(2).to_broadcast([128, 8, 128]), op=ALU.mult), [pX, colv], [snT[slot]])
                    sc.dma('sp', ssdnT_d[lc], snT[slot][:].rearrange("p j t -> p (j t)"), [snT[slot]], [D_ssdnT], snT[slot])

                lat_AU(NCX)
                lat_A(NCX, 0)
                for i, ch in enumerate(range(NCX, NCH)):
                    lat_B(ch, i % 2)
                    lat_B1b(ch, i % 2)
                    if ch + 1 < NCH:
                        lat_AU(ch + 1, (0,))
                    if i >= 1:
                        lat_B2(ch - 1, (i - 1) % 2)
                    if ch + 1 < NCH:
                        lat_AU(ch + 1, (1,))
                        lat_A(ch + 1, (i + 1) % 2)
                    lat_B1c(ch, i % 2)
                lat_B2(NCH - 1, (NCL - 1) % 2)
                sc.barrier()
            xstk.close()

            with contextlib.ExitStack() as st:
                wq_sb = sb(st, "wq_sb", [128, 3, NH * 96], BF16)
                wqs_sb = sb(st, "wqs_sb", [128, 3, NH * 96], BF16)
                wk_sb = sb(st, "wk_sb", [128, 2, NH * 64], BF16)
                wv_sb = sb(st, "wv_sb", [128, 2, NH * 64], BF16)
                c1 = sb(st, "c1a", [96, S], BF16)
                c2 = sb(st, "c2a", [96, S], BF16)
                Vaug = sb(st, "Vaug", [128, NCH, NH, 128], BF16)
                qT = [sb(st, "qT%d" % i, [96, S], BF16) for i in range(2)]
                kT = [sb(st, "kT%d" % i, [96, L], BF16) for i in range(2)]
                qa = sb(st, "qa", [96, 512], F32)
                qb_ = sb(st, "qb_", [96, 512], F32)
                NPT = 4
                PT = [sb(st, "PT%d" % i, [128, 1024], BF16) for i in range(NPT)]
                rc = sb(st, "rc", [64, 512], F32)
                ao = [sb(st, "ao%d" % i, [64, S], BF16) for i in range(2)]
                pSs = [ps(st, "pSs%d" % i, [128, 1024], F32) for i in range(2)]
                pP = ps(st, "pP", [128, 1024], F32)
                pO = [ps(st, "pO%d" % i, [128, 512], F32) for i in range(2)]
                sc.dma('sp', wq_sb[:], wq_b.rearrange("(k p) c -> p k c", p=128), [D_w["wq_b"]], [wq_sb], wq_sb)
                sc.dma('sp', wqs_sb[:], wqs_b.rearrange("(k p) c -> p k c", p=128), [D_w["wqs_b"]], [wqs_sb], wqs_sb)
                sc.dma('sp', wk_sb[:], wk_b.rearrange("(k p) c -> p k c", p=128), [D_w["wk_b"]], [wk_sb], wk_sb)
                sc.dma('sp', wv_sb[:], wv_b.rearrange("(k p) c -> p k c", p=128), [D_w["wv_b"]], [wv_sb], wv_sb)
                sc.dma('pool', c1[:], c1_d[:, CTX:L], [], [c1], c1)
                sc.dma('pool', c2[:], c2_d[:, CTX:L], [], [c2], c2)
                for kt0 in range(0, NCH, 3):
                    pool(lambda e, kt0=kt0: e.memset(Vaug[:, kt0:min(NCH, kt0 + 3), :, 64:128], 1.0), [], [Vaug])
                kblocks = [(i * 512, min(512, L - i * 512)) for i in range((L + 511) // 512)]
                for kt in range(NCH):
                    for half in range(2):
                        p = pSs[half]
                        for k in range(2):
                            mm(p[:, 0:512], ckvnT[:, k, kt * 128:(kt + 1) * 128], wv_sb[:, k, half * 512:(half + 1) * 512], k == 0, k == 1, [ckvnT, wv_sb], [p])
                        dve(lambda e, p=p, kt=kt, half=half: e.tensor_copy(out=Vaug[:, kt, half * 8:(half + 1) * 8, 0:64], in_=p[:, 0:512].rearrange("p (h q) -> p h q", h=8)), [p], [Vaug])

                def prep_pieces(h):
                    q_ = qT[h % 2]; k_ = kT[h % 2]
                    pcs = []
                    for qb in range(S // 512):
                        def f(qb=qb):
                            for k in range(3):
                                mm(pP[0:96, 0:512], wq_sb[:, k, h * 96:(h + 1) * 96], cqnT[:, k, qb * 512:(qb + 1) * 512], k == 0, k == 2, [wq_sb, cqnT], [pP])
                            for k in range(3):
                                mm(pP[0:96, 512:1024], wqs_sb[:, k, h * 96:(h + 1) * 96], cqnT[:, k, qb * 512:(qb + 1) * 512], k == 0, k == 2, [wqs_sb, cqnT], [pP])
                            dve(lambda e: e.tensor_tensor(out=qa[:], in0=pP[0:96, 0:512], in1=c1[:, qb * 512:(qb + 1) * 512], op=ALU.mult), [pP, c1], [qa])
                            dve(lambda e: e.tensor_tensor(out=qb_[:], in0=pP[0:96, 512:1024], in1=c2[:, qb * 512:(qb + 1) * 512], op=ALU.mult), [pP, c2], [qb_])
                            pool(lambda e: e.tensor_tensor(out=q_[:, qb * 512:(qb + 1) * 512], in0=qa[:], in1=qb_[:], op=ALU.add), [qa, qb_], [q_])
                        pcs.append(f)
                    for (k0, kn) in kblocks:
                        def f(k0=k0, kn=kn):
                            for k in range(2):
                                mm(pP[0:64, 0:kn], wk_sb[:, k, h * 64:(h + 1) * 64], ckvnT[:, k, k0:k0 + kn], k == 0, k == 1, [wk_sb, ckvnT], [pP])
                            dve(lambda e: e.tensor_copy(out=k_[0:64, k0:k0 + kn], in_=pP[0:64, 0:kn]), [pP], [k_])
                        pcs.append(f)
                    pcs.append(lambda: pool(lambda e: e.tensor_copy(out=k_[64:96, :], in_=krT[64:96, :]), [krT], [k_]))
                    return pcs

                for f in prep_pieces(0):
                    f()
                ngr = NCH // 2
                units = [(h, qb) for h in range(NH) for qb in range(S // 512)]
                groups = [(h, qb, gr) for (h, qb) in units for gr in range(ngr)]
                pending = None
                gi = 0
                pend_prep = []

                def emit_pv(item):
                    h, qb, gr, pt_ = item
                    po = pO[(h * (S // 512) + qb) % 2]
                    ao_ = ao[h % 2]
                    for u in range(2):
                        kt = gr * 2 + u
                        mm(po[:], Vaug[:, kt, h, :], pt_[:, u * 512:(u + 1) * 512], kt == 0, kt == NCH - 1, [Vaug, pt_], [po], sig=(u == 1))
                    if gr == ngr - 1:
                        dve(lambda e: e.reciprocal(out=rc[:], in_=po[64:128, :]), [po], [rc])
                        dve(lambda e: e.tensor_tensor(out=ao_[:, qb * 512:(qb + 1) * 512], in0=po[0:64, :], in1=rc[:], op=ALU.mult), [po, rc], [ao_])
                        if qb == S // 512 - 1:
                            sc.dma('sp', attnT_d[:, (h % 2) * 64:(h % 2) * 64 + 64, (h // 2) * 128:(h // 2 + 1) * 128].rearrange("t p c -> p t c"), ao_[:].rearrange("p (t c) -> p t c", c=128), [ao_], [D_attnT], ao_)

                for (h, qb, gr) in groups:
                    if qb == 0 and gr == 0:
                        while pend_prep:
                            pend_prep.pop(0)()
                        if h + 1 < NH:
                            pend_prep = prep_pieces(h + 1)
                    q_ = qT[h % 2]; k_ = kT[h % 2]
                    p = pSs[gi % 2]; pt_ = PT[gi % NPT]
                    for u in range(2):
                        kt = gr * 2 + u
                        mm(p[:, u * 512:(u + 1) * 512], k_[:, kt * 128:(kt + 1) * 128], q_[:, qb * 512:(qb + 1) * 512], True, True, [k_, q_], [p], sig=(u == 1))
                    act(lambda e, p=p, pt_=pt_: e.activation(out=pt_[:], in_=p[:], func=AF.Exp, scale=SCALE), [p], [pt_])
                    if pending is not None:
                        emit_pv(pending)
                    pending = (h, qb, gr, pt_)
                    gi += 1
                    if pend_prep and gi % 3 == 0:
                        pend_prep.pop(0)()
                emit_pv(pending)
                sc.barrier()

            bs.close()
            fs = contextlib.ExitStack()
            es.enter_context(fs)
            h2T = sb(fs, "h2T", [128, 8, S], BF16)
            h2Tk = [Buf(h2T.t, "h2T%d" % k) for k in range(8)]
            with contextlib.ExitStack() as st:
                woa = sb(st, "woa", [128, 8, D], BF16)
                wos = sb(st, "wos", [128, 8, D], BF16)
                aT = [sb(st, "aT%d" % i, [128, 8, 128], BF16) for i in range(3)]
                sT = [sb(st, "sT%d" % i, [128, 8, 128], BF16) for i in range(3)]
                xt = [sb(st, "xt5_%d" % i, [128, D], F32) for i in range(3)]
                x1 = [sb(st, "x1_%d" % i, [128, D], F32) for i in range(2)]
                tq = sb(st, "tq", [128, D], F32)
                xs2 = [sb(st, "xs2_%d" % i, [128, D], BF16) for i in range(2)]
                junk5 = sb(st, "junk5", [128, D], BF16)
                junk6 = sb(st, "junk6", [128, D], BF16)
                tmp5 = [sb(st, "tmp5_%d" % i, [128, 8, 128], BF16) for i in range(2)]
                ss5 = [sb(st, "ss5_%d" % i, [128, 2], F32) for i in range(2)]
                ss6 = [sb(st, "ss6_%d" % i, [128, 2], F32) for i in range(2)]
                pM = [ps(st, "pM%d" % i, [128, 1024], F32) for i in range(2)]
                pT5 = [ps(st, "pT5_%d" % i, [128, 8, 128], BF16) for i in range(2)]
                GW2 = sb(st, "GW2", [128, D], F32)
                rw = sb(st, "rw5", [128, D], F32)
                sc.dma('sp', rw[:], rows_d[0:1, :].partition_broadcast(128), [], [rw], rw)
                sc.dma('sp', GW2[:], modv_d[b:b + 1, 2 * D:3 * D].partition_broadcast(128), [D_modv], [GW2], GW2)
                dve(lambda e: e.tensor_tensor(out=GW2[:], in0=GW2[:], in1=rw[:], op=ALU.mult), [GW2, rw], [GW2])
                sc.dma('sp', woa[:], w_out_b[0:1024, :].rearrange("(k p) c -> p k c", p=128), [D_w["w_out_b"]], [woa], woa)
                sc.dma('sp', wos[:], w_out_b[1024:2048, :].rearrange("(k p) c -> p k c", p=128), [D_w["w_out_b"]], [wos], wos)

                def p5_loads(t):
                    a_ = aT[t % 3]; s_ = sT[t % 3]; x_ = xt[t % 3]
                    sc.dma('sp', a_[:].rearrange("p h t -> p (h t)"), attnT_d[t], [D_attnT], [a_], a_)
                    sc.dma('sp', s_[:].rearrange("p j t -> p (j t)"), ssdnT_d[t], [D_ssdnT], [s_], s_)
                    sc.dma('sp', x_[:], x_d[b, t * 128:(t + 1) * 128, :], [], [x_], x_)

                p5_loads(0)

                def p5_front(t):
                    a_ = aT[t % 3]; s_ = sT[t % 3]; x_ = xt[t % 3]; x1_ = x1[t % 2]; p = pM[t % 2]
                    s5 = ss5[t % 2]; s6 = ss6[t % 2]; xs_ = xs2[t % 2]
                    if t + 1 < NCL:
                        p5_loads(t + 1)
                    for nb in range(2):
                        for hh in range(8):
                            mm(p[:, nb * 512:(nb + 1) * 512], a_[:, hh, :], woa[:, hh, nb * 512:(nb + 1) * 512], hh == 0, False, [a_, woa], [p], sig=False)
                        for k in range(8):
                            mm(p[:, nb * 512:(nb + 1) * 512], s_[:, k, :], wos[:, k, nb * 512:(nb + 1) * 512], False, k == 7, [s_, wos], [p], sig=(k == 7))
                    act(lambda e: e.activation(out=junk5[:], in_=p[:], func=AF.Square, accum_out=s5[:, 0:1]), [p], [junk5, s5])
                    rstd_from_ss(s5, D)
                    dve(lambda e: e.scalar_tensor_tensor(out=tq[:], in0=p[:], scalar=s5[:, 1:2], in1=GW2[:], op0=ALU.mult, op1=ALU.mult), [p, s5, GW2], [tq])
                    pool(lambda e: e.tensor_tensor(out=x1_[:], in0=tq[:], in1=x_[:], op=ALU.add), [tq, x_], [x1_])
                    sc.dma('sp', x1_d[t * 128:(t + 1) * 128, :], x1_[:], [x1_], [D_x1], x1_)
                    act(lambda e: e.activation(out=junk6[:], in_=x1_[:], func=AF.Square, accum_out=s6[:, 0:1]), [x1_], [junk6, s6])
                    rstd_from_ss(s6, D)
                    dve(lambda e: e.tensor_scalar(out=xs_[:], in0=x1_[:], scalar1=s6[:, 1:2], scalar2=None, op0=ALU.mult), [x1_, s6], [xs_])

                def p5_back(t):
                    xs_ = xs2[t % 2]; pt = pT5[t % 2]
                    for k in range(8):
                        tr(pt[:, k, :], xs_[:, k * 128:(k + 1) * 128], identb[:], [xs_, identb], [pt], sig=(k == 7))
                    tm_ = tmp5[t % 2]
                    dve(lambda e: e.tensor_tensor(out=tm_[:], in0=pt[:], in1=G4[:].unsqueeze(2).to_broadcast([128, 8, 128]), op=ALU.mult), [pt, G4], [tm_])
                    dve(lambda e: e.tensor_tensor(out=h2T[:, :, t * 128:(t + 1) * 128], in0=tm_[:], in1=modT[0][:, 24:32].unsqueeze(2).to_broadcast([128, 8, 128]), op=ALU.add),
                        [tm_, modT[0]], h2Tk)

                for t in range(NCL + 1):
                    if t < NCL:
                        p5_front(t)
                    if t >= 1:
                        p5_back(t - 1)
                sc.barrier()

            actT = sb(fs, "actT", [128, NFF, S], BF16)
            actTj = [Buf(actT.t, "actT%d" % j) for j in range(NFF)]
            with contextlib.ExitStack() as st:
                wg = [sb(st, "wg%d" % i, [128, 8, 128], BF16) for i in range(2)]
                wv_ = [sb(st, "wvv%d" % i, [128, 8, 128], BF16) for i in range(2)]
                Gt = [sb(st, "Gt%d" % i, [128, S + 2], F32) for i in range(2)]
                Vv = [sb(st, "Vv%d" % i, [128, S], BF16) for i in range(2)]
                cv = [sb(st, "cv%d" % i, [128, S], F32) for i in range(2)]
                ge = [sb(st, "ge%d" % i, [128, S], BF16) for i in range(2)]
                pG = [ps(st, "pG%d" % i, [128, 512], F32) for i in range(4)]
                pV = [ps(st, "pV%d" % i, [128, 512], F32) for i in range(4)]
                for i in range(2):
                    dve(lambda e, i=i: e.memset(Gt[i][:], 0.0), [], [Gt[i]])
                pidx = 0
                for j in range(NFF):
                    g_ = wg[j % 2]; v_ = wv_[j % 2]; G_ = Gt[j % 2]; V_ = Vv[j % 2]; c_ = cv[j % 2]; e_ = ge[j % 2]
                    sc.dma('sp', g_[:], w_up_b[:, j * 128:(j + 1) * 128].rearrange("(k p) c -> p k c", p=128), [D_w["w_up_b"]], [g_], g_)
                    sc.dma('sp', v_[:], w_up_b[:, DFF + j * 128:DFF + (j + 1) * 128].rearrange("(k p) c -> p k c", p=128), [D_w["w_up_b"]], [v_], v_)
                    for tb in range(S // 512):
                        pg = pG[pidx % 4]; pv_ = pV[pidx % 4]; pidx += 1
                        for k in range(8):
                            mm(pg[:], g_[:, k, :], h2T[:, k, tb * 512:(tb + 1) * 512], k == 0, k == 7, [g_, h2Tk[k]], [pg])
                        for k in range(8):
                            mm(pv_[:], v_[:, k, :], h2T[:, k, tb * 512:(tb + 1) * 512], k == 0, k == 7, [v_, h2Tk[k]], [pv_])
                        act(lambda e, pg=pg, G_=G_, tb=tb: e.activation(out=G_[:, 1 + tb * 512:1 + (tb + 1) * 512], in_=pg[:], func=AF.Copy), [pg], [G_])
                        dve(lambda e, pv_=pv_, V_=V_, tb=tb: e.tensor_copy(out=V_[:, tb * 512:(tb + 1) * 512], in_=pv_[:]), [pv_], [V_])
                    en = 'dve'
                    en2 = 'pool' if j % 2 == 0 else 'dve'
                    w0 = colv[:, R_FCW + j:R_FCW + j + 1]; w1 = colv[:, R_FCW + 22 + j:R_FCW + 23 + j]; w2 = colv[:, R_FCW + 44 + j:R_FCW + 45 + j]
                    bc = colv[:, R_FCB + j:R_FCB + j + 1]
                    act(lambda e, G_=G_, c_=c_, w0=w0, bc=bc: e.activation(out=c_[:], in_=G_[:, 0:S], func=AF.Identity, scale=w0, bias=bc), [G_, colv], [c_])
                    sc.op(en, lambda e, G_=G_, c_=c_, w1=w1: e.scalar_tensor_tensor(out=c_[:], in0=G_[:, 1:S + 1], scalar=w1, in1=c_[:], op0=ALU.mult, op1=ALU.add), [G_, colv, c_], [c_])
                    sc.op(en, lambda e, G_=G_, c_=c_, w2=w2: e.scalar_tensor_tensor(out=c_[:], in0=G_[:, 2:S + 2], scalar=w2, in1=c_[:], op0=ALU.mult, op1=ALU.add), [G_, colv, c_], [c_])
                    act(lambda e, c_=c_, e_=e_: e.activation(out=e_[:], in_=c_[:], func=AF.Gelu), [c_], [e_])
                    sc.op(en2, lambda e, e_=e_, V_=V_, j=j: e.tensor_tensor(out=actT[:, j, :], in0=e_[:], in1=V_[:], op=ALU.mult), [e_, V_], [actTj[j]])
                sc.barrier()

            with contextlib.ExitStack() as st:
                wd = sb(st, "wd", [128, NFF, D], BF16)
                x1 = [sb(st, "x1r%d" % i, [128, D], F32) for i in range(2)]
                ot = [sb(st, "ot%d" % i, [128, D], F32) for i in range(2)]
                tq = sb(st, "tq7", [128, D], F32)
                junk7 = sb(st, "junk7", [128, D], BF16)
                ss7 = sb(st, "ss7", [128, 2], F32)
                pM = [ps(st, "pM7_%d" % i, [128, 1024], F32) for i in range(2)]
                GW5 = sb(st, "GW5", [128, D], F32)
                rw = sb(st, "rw7", [128, D], F32)
                sc.dma('sp', rw[:], rows_d[1:2, :].partition_broadcast(128), [], [rw], rw)
                sc.dma('sp', GW5[:], modv_d[b:b + 1, 5 * D:6 * D].partition_broadcast(128), [D_modv], [GW5], GW5)
                dve(lambda e: e.tensor_tensor(out=GW5[:], in0=GW5[:], in1=rw[:], op=ALU.mult), [GW5, rw], [GW5])
                sc.dma('sp', wd[:], w_down_b.rearrange("(k p) c -> p k c", p=128), [D_w["w_down_b"]], [wd], wd)
                for t in range(NCL):
                    x1_ = x1[t % 2]; o_ = ot[t % 2]; p = pM[t % 2]
                    sc.dma('sp', x1_[:], x1_d[t * 128:(t + 1) * 128, :], [D_x1], [x1_], x1_)
                    for nb in range(2):
                        for j in range(NFF):
                            mm(p[:, nb * 512:(nb + 1) * 512], actT[:, j, t * 128:(t + 1) * 128], wd[:, j, nb * 512:(nb + 1) * 512], j == 0, j == NFF - 1, [actTj[j], wd], [p])
                    act(lambda e, p=p: e.activation(out=junk7[:], in_=p[:], func=AF.Square, accum_out=ss7[:, 0:1]), [p], [junk7, ss7])
                    rstd_from_ss(ss7, D)
                    dve(lambda e, p=p: e.scalar_tensor_tensor(out=tq[:], in0=p[:], scalar=ss7[:, 1:2], in1=GW5[:], op0=ALU.mult, op1=ALU.mult), [p, ss7, GW5], [tq])
                    pool(lambda e, x1_=x1_, o_=o_: e.tensor_tensor(out=o_[:], in0=tq[:], in1=x1_[:], op=ALU.add), [tq, x1_], [o_])
                    sc.dma('sp', out_d[b, t * 128:(t + 1) * 128, :], o_[:], [o_], [D_out], o_)
                sc.barrier()
            fs.close()
            ms.close()
        sc.barrier()
    return nc


def _rope_tables(S, CTX):
    GRID_W = 64
    n_rows = S // GRID_W
    row = np.repeat(np.arange(n_rows), GRID_W).astype(np.float32)
    col = np.tile(np.arange(GRID_W), n_rows).astype(np.float32)
    axis_dim = 16
    inv_freq = (10000.0 ** (-np.arange(0, axis_dim, 2, dtype=np.float32) / axis_dim)).astype(np.float32)
    ang_r = row[:, None] * inv_freq
    ang_c = col[:, None] * inv_freq
    ang = np.concatenate([ang_r, ang_r, ang_c, ang_c], axis=-1).astype(np.float32)
    cos, sin = np.cos(ang), np.sin(ang)
    sgn = np.array([-1.0] * 8 + [1.0] * 8 + [-1.0] * 8 + [1.0] * 8, np.float32)
    L = CTX + S
    c1 = np.ones((96, L), np.float32)
    c2 = np.zeros((96, L), np.float32)
    c1[64:96, CTX:] = cos.T
    c2[64:96, CTX:] = (sin * sgn).T
    return c1, c2


_SWAP = np.concatenate([np.arange(8, 16), np.arange(0, 8), np.arange(24, 32), np.arange(16, 24)])


def prepare_shared(inp, S, CTX):
    f = lambda a: np.ascontiguousarray(np.asarray(a, dtype=np.float32))
    w_in = f(inp["w_in"][0])
    kr = w_in[:, 640:672]
    wkr = np.zeros((D, 192), np.float32)
    wkr[:, 64:96] = kr
    wkr[:, 160:192] = kr[:, _SWAP]
    wq = f(inp["w_q_up"][0])
    wq3 = wq.reshape(384, NH, 96)
    wqs = np.zeros_like(wq3)
    wqs[:, :, 64:96] = wq3[:, :, 64:96][:, :, _SWAP]
    wkv = f(inp["w_kv_up"][0]).reshape(256, NH, 128)
    wk = np.ascontiguousarray(wkv[:, :, 0:64]).reshape(256, NH * 64)
    wv = np.ascontiguousarray(wkv[:, :, 64:128]).reshape(256, NH * 64)
    vecs = np.zeros((256, 128), np.float32)
    vecs[0:8] = f(inp["mix_pre_norm"][0]).reshape(8, 128)
    vecs[8:11] = f(inp["q_norm"][0]).reshape(3, 128)
    vecs[11:13] = f(inp["kv_norm"][0]).reshape(2, 128)
    vecs[13:73] = f(inp["ssd_conv_w"][0]).reshape(60, 128)
    vecs[73:85] = f(inp["ssd_conv_b"][0]).reshape(12, 128)
    vecs[85:93] = f(inp["ssd_norm"][0]).reshape(8, 128)
    vecs[93:101] = f(inp["ffn_pre_norm"][0]).reshape(8, 128)
    vecs[101:167] = f(inp["ffn_conv_w"][0]).reshape(66, 128)
    vecs[167:189] = f(inp["ffn_conv_b"][0]).reshape(22, 128)
    vecs[189:197] = np.repeat(f(inp["ssd_d"][0]), 64).reshape(8, 128)
    rows = np.zeros((4, D), np.float32)
    rows[0] = f(inp["mix_post_norm"][0])
    rows[1] = f(inp["ffn_post_norm"][0])
    small = np.zeros((1, 96), np.float32)
    small[0, 0:32] = f(inp["ssd_a_log"][0]).reshape(32)
    small[0, 32:64] = f(inp["ssd_dt_bias"][0]).reshape(32)
    small[0, 64:80] = f(inp["ssd_d"][0])
    c1, c2 = _rope_tables(S, CTX)
    ii = np.arange(128)
    return dict(
        w_mod=f(inp["w_mod"][0]), b_mod=f(inp["b_mod"][0]).reshape(1, -1), w_in=w_in, wkr=wkr,
        wq=np.ascontiguousarray(wq), wqs=np.ascontiguousarray(wqs.reshape(384, NH * 96)), wk=wk, wv=wv,
        w_out=f(inp["w_out"][0]), w_up=f(inp["w_up"][0]), w_down=f(inp["w_down"][0]),
        vecs=vecs, rows=rows, small=small, ident=np.eye(128, dtype=np.float32),
        uf=(ii[:, None] <= ii[None, :]).astype(np.float32), ub=(ii[:, None] >= ii[None, :]).astype(np.float32),
        c1=c1, c2=c2)


def make_in_maps(inp, n_cores, NB):
    x = np.asarray(inp["x"], np.float32)
    ctx = np.asarray(inp["ctx"], np.float32)
    c = np.asarray(inp["c"], np.float32)
    c_ctx = np.asarray(inp["c_ctx"], np.float32).reshape(1, D)
    S, CTX = x.shape[1], ctx.shape[1]
    shared = prepare_shared(inp, S, CTX)
    maps = []
    for i in range(n_cores):
        m = dict(shared)
        m["x"] = np.ascontiguousarray(x[i * NB:(i + 1) * NB])
        m["ctx"] = np.ascontiguousarray(ctx[i * NB:(i + 1) * NB])
        m["cc"] = np.ascontiguousarray(np.concatenate([c[i * NB:(i + 1) * NB], c_ctx], axis=0))
        maps.append(m)
    return maps


def kernel(**inputs):
    n_cores = 8
    x = np.asarray(inputs["x"])
    B, S, _ = x.shape
    CTX = np.asarray(inputs["ctx"]).shape[1]
    NB = B // n_cores
    nc = build_nc(NB, S, CTX)
    in_maps = make_in_maps(inputs, n_cores, NB)
    res = run_bass_kernel_spmd(nc, in_maps, core_ids=list(range(n_cores)))
    out = np.concatenate([np.asarray(r["out"], dtype=np.float32) for r in res.results], axis=0)
    return out
```

```python
import contextlib
import numpy as np
import concourse.bass as bass
import concourse.mybir as mybir
from concourse.bass_utils import run_bass_kernel_spmd

F32 = mybir.dt.float32
BF16 = mybir.dt.bfloat16
AF = mybir.ActivationFunctionType
ALU = mybir.AluOpType

D = 1024
NH = 16
DFF = 2816
NFF = DFF // 128
INW = 3264
EPS = 1e-6
SCALE = 96 ** -0.5


class Buf:
    def __init__(self, t, name):
        self.t = t
        self.name = name
        self.lw = None
        self.rd = {}
        self.psum = False
        self.xacc = None

    def __getitem__(self, k):
        return self.t[k]


class Sched:
    def __init__(self, nc, es):
        self.nc = nc
        self.es = es
        self.eng = {'pe': nc.tensor, 'act': nc.scalar, 'dve': nc.vector, 'pool': nc.gpsimd, 'sp': nc.sync}
        self.sem = {e: es.enter_context(nc.semaphore('s_' + e)) for e in ['pe', 'act', 'dve', 'pool']}
        self.cnt = {e: 0 for e in self.sem}
        self.known = {e: {} for e in self.eng}
        self.dtot = {}
        self.dsem = {}
        self.free_sems = {'sp': [], 'pool': []}
        self.all_dsems = []
        self.nsem = 0
        self.pinned = set()

    def _wait(self, e, tok):
        sem, val, src = tok
        if src == 'dma':
            val = max(val, self.dtot[id(sem)])
        k = self.known[e]
        if k.get(id(sem), 0) >= val:
            return
        self.eng[e].wait_ge(sem, val)
        k[id(sem)] = val

    def _deps(self, e, reads, writes, is_dma=False):
        toks = []
        for b in reads:
            if b.lw is not None:
                toks.append((b.lw, 'raw'))
        for b in writes:
            if b.lw is not None:
                toks.append((b.lw, 'waw'))
            for r in b.rd.values():
                toks.append((r, 'war'))
        for tok, kind in toks:
            if (not is_dma) and tok[2] == e and e == 'pe':
                continue
            self._wait(e, tok)

    def _update(self, tok, reads, writes):
        for b in reads:
            b.rd[id(tok[0])] = tok
        for b in writes:
            b.lw = tok
            b.rd = {}

    def op(self, e, fn, reads=(), writes=(), sig=True):
        self._deps(e, reads, writes)
        if e != 'pe':
            for b in list(reads) + list(writes):
                if b.psum and b.xacc is not None and b.xacc[2] != e:
                    self._wait(e, b.xacc)
        ins = fn(self.eng[e])
        if sig:
            self.cnt[e] += 1
            ins.then_inc(self.sem[e], 1)
            tok = (self.sem[e], self.cnt[e], e)
        else:
            tok = (self.sem[e], self.cnt[e] + 1, e)
        self._update(tok, reads, writes)
        if e != 'pe':
            for b in list(reads) + list(writes):
                if b.psum:
                    b.xacc = tok

    def _get_dsem(self, owner, q):
        key = (id(owner), q)
        if key not in self.dsem:
            if self.free_sems[q]:
                sem = self.free_sems[q].pop()
            else:
                self.nsem += 1
                sem = self.es.enter_context(self.nc.semaphore('d%s%d' % (q, self.nsem)))
                self.dtot[id(sem)] = 0
                self.all_dsems.append(sem)
            self.dsem[key] = sem
        return self.dsem[key]

    def dma(self, q, out, in_, reads, writes, owner, **kw):
        self._deps(q, reads, writes, is_dma=True)
        sem = self._get_dsem(owner, q)
        self.dtot[id(sem)] += 16
        self.eng[q].dma_start(out=out, in_=in_, **kw).then_inc(sem, 16)
        tok = (sem, self.dtot[id(sem)], 'dma')
        self._update(tok, reads, writes)

    def barrier(self, release=True):
        for e in self.eng:
            for e2 in self.sem:
                if e2 != e and self.cnt[e2] > 0:
                    self._wait(e, (self.sem[e2], self.cnt[e2], e2))
            for sem in self.all_dsems:
                if self.dtot[id(sem)] > 0 and id(sem) not in self.pinned:
                    self._wait(e, (sem, self.dtot[id(sem)], 'dma'))
        if release:
            keep = {}
            for (oid, q), sem in self.dsem.items():
                if id(sem) in self.pinned:
                    keep[(oid, q)] = sem
                else:
                    self.free_sems[q].append(sem)
            self.dsem = keep


def build_nc(NB, S, CTX):
    L = CTX + S
    NCX = CTX // 128
    NCL = S // 128
    NCH = NCX + NCL
    XOFF_C = 2
    XOFF_L = CTX + 6
    XW = CTX + S + 8
    nc = bass.Bass("TRN2", target_bir_lowering=False)

    def din(name, shape, dt=F32):
        return nc.dram_tensor(name, list(shape), dt, kind="ExternalInput").ap()

    def dscr(name, shape, dt):
        return nc.dram_tensor(name, list(shape), dt, kind="Internal").ap()

    x_d = din("x", [NB, S, D])
    ctx_d = din("ctx", [NB, CTX, D])
    cc_d = din("cc", [NB + 1, D])
    w_mod_d = din("w_mod", [D, 6 * D])
    b_mod_d = din("b_mod", [1, 6 * D])
    w_in_d = din("w_in", [D, INW])
    wkr_d = din("wkr", [D, 192])
    wq_d = din("wq", [384, NH * 96])
    wqs_d = din("wqs", [384, NH * 96])
    wk_d = din("wk", [256, NH * 64])
    wv_d = din("wv", [256, NH * 64])
    w_out_d = din("w_out", [2048, D])
    w_up_d = din("w_up", [D, 2 * DFF])
    w_down_d = din("w_down", [DFF, D])
    NV = 256
    vecs_d = din("vecs", [NV, 128])
    rows_d = din("rows", [4, D])
    small_d = din("small", [1, 96])
    ident_d = din("ident", [128, 128])
    uf_d = din("uf", [128, 128])
    ub_d = din("ub", [128, 128])
    c1_d = din("c1", [96, L])
    c2_d = din("c2", [96, L])
    out_d = nc.dram_tensor("out", [NB, S, D], F32, kind="ExternalOutput").ap()

    w_in_b = dscr("w_in_b", [D, INW], BF16)
    wkr_b = dscr("wkr_b", [D, 192], BF16)
    wq_b = dscr("wq_b", [384, NH * 96], BF16)
    wqs_b = dscr("wqs_b", [384, NH * 96], BF16)
    wk_b = dscr("wk_b", [256, NH * 64], BF16)
    wv_b = dscr("wv_b", [256, NH * 64], BF16)
    w_out_b = dscr("w_out_b", [2048, D], BF16)
    w_up_b = dscr("w_up_b", [D, 2 * DFF], BF16)
    w_down_b = dscr("w_down_b", [DFF, D], BF16)
    modv_d = dscr("modv", [NB + 1, 6 * D], F32)
    zs_d = dscr("zs", [S, D], BF16)
    x1_d = dscr("x1", [S, D], F32)
    attnT_d = dscr("attnT", [S // 128, 128, 8 * 128], BF16)
    ssdnT_d = dscr("ssdnT", [S // 128, 128, 8 * 128], BF16)
    sbst_d = dscr("sbst", [S // 128, 128, D], BF16)

    es = contextlib.ExitStack()
    with es:
        sc = Sched(nc, es)
        uid = [0]

        def sb(st, name, shape, dt):
            uid[0] += 1
            return Buf(st.enter_context(nc.sbuf_tensor("%s_%d" % (name, uid[0]), list(shape), dt)), name)

        def ps(st, name, shape, dt=F32):
            uid[0] += 1
            bf = Buf(st.enter_context(nc.psum_tensor("%s_%d" % (name, uid[0]), list(shape), dt)), name)
            bf.psum = True
            return bf

        def dbuf(ap, name):
            return Buf(ap, name)

        D_w = {n: dbuf(None, n) for n in ["w_in_b", "wkr_b", "wq_b", "wqs_b", "wk_b", "wv_b", "w_out_b", "w_up_b", "w_down_b"]}
        D_modv = dbuf(None, "modv")
        D_zs = dbuf(None, "zs")
        D_x1 = dbuf(None, "x1")
        D_attnT = dbuf(None, "attnT")
        D_ssdnT = dbuf(None, "ssdnT")
        D_out = dbuf(None, "out")

        def act(fn, reads, writes):
            sc.op('act', fn, reads, writes)

        def dve(fn, reads, writes):
            sc.op('dve', fn, reads, writes)

        def pool(fn, reads, writes):
            sc.op('pool', fn, reads, writes)

        def mm(out, lhsT, rhs, start, stop, reads, writes, sig=None):
            if sig is None:
                sig = stop
            sc.op('pe', lambda e: e.matmul(out, lhsT=lhsT, rhs=rhs, start=start, stop=stop), reads, writes, sig=sig)

        def tr(out, in_, idt, reads, writes, sig=True):
            sc.op('pe', lambda e: e.transpose(out, in_, idt), reads, writes, sig=sig)

        def rstd_from_ss(ssb, n):
            act(lambda e: e.activation(out=ssb[:, 1:2], in_=ssb[:, 0:1], func=AF.Ln, scale=1.0 / n, bias=epsb[:, 0:1]), [ssb, epsb], [ssb])
            act(lambda e: e.activation(out=ssb[:, 1:2], in_=ssb[:, 1:2], func=AF.Exp, scale=-0.5), [ssb], [ssb])

        gs = contextlib.ExitStack()
        es.enter_context(gs)
        identb = sb(gs, "identb", [128, 128], BF16)
        identf = sb(gs, "identf", [128, 128], F32)
        uf = sb(gs, "uf", [128, 128], F32)
        ub = sb(gs, "ub", [128, 128], F32)
        ufb = sb(gs, "ufb", [128, 128], BF16)
        ubb = sb(gs, "ubb", [128, 128], BF16)
        onesf = sb(gs, "onesf", [128, 128], F32)
        epsb = sb(gs, "epsb", [128, 1], F32)
        colv = sb(gs, "colv", [128, NV], F32)
        smallb = sb(gs, "smallb", [128, 96], F32)
        diagD = sb(gs, "diagD", [128, 8, 128], BF16)
        sc.dma('pool', identb[:], ident_d, [], [identb], identb)
        sc.dma('pool', ufb[:], uf_d, [], [ufb], ufb)
        sc.dma('pool', ubb[:], ub_d, [], [ubb], ubb)
        sc.dma('sp', identf[:], ident_d, [], [identf], identf)
        sc.dma('sp', uf[:], uf_d, [], [uf], uf)
        sc.dma('sp', ub[:], ub_d, [], [ub], ub)
        sc.dma('sp', smallb[:], small_d.partition_broadcast(128), [], [smallb], smallb)
        dve(lambda e: e.memset(onesf[:], 1.0), [], [onesf])
        dve(lambda e: e.memset(epsb[:], EPS), [], [epsb])
        act(lambda e: e.activation(out=smallb[:, 0:32], in_=smallb[:, 0:32], func=AF.Exp), [smallb], [smallb])
        dve(lambda e: e.tensor_scalar(out=smallb[:, 0:32], in0=smallb[:, 0:32], scalar1=-1.0, scalar2=None, op0=ALU.mult), [smallb], [smallb])

        R_MIXPRE, R_QN, R_KVN, R_CW, R_CB, R_SN, R_FPRE, R_FCW, R_FCB, R_DEXP = 0, 8, 11, 13, 73, 85, 93, 101, 167, 189
        with contextlib.ExitStack() as st:
            vt = sb(st, "vt", [128, 2, 128], F32)
            pv = ps(st, "pv", [128, 2, 128], F32)
            sc.dma('sp', vt[:], vecs_d.rearrange("(a p) c -> p a c", p=128), [], [vt], vt)
            for a in range(2):
                tr(pv[:, a, :], vt[:, a, :], identf[:], [vt, identf], [pv])
            dve(lambda e: e.tensor_copy(out=colv[:], in_=pv[:].rearrange("p a c -> p (a c)")), [pv], [colv])
            for j in range(8):
                dve(lambda e, j=j: e.tensor_scalar(out=diagD[:, j, :], in0=identf[:], scalar1=colv[:, R_DEXP + j:R_DEXP + j + 1], scalar2=None, op0=ALU.mult),
                    [identf, colv], [diagD])
            sc.barrier()

        castlist = [(w_in_b, w_in_d, "w_in_b"), (wkr_b, wkr_d, "wkr_b"), (wq_b, wq_d, "wq_b"), (wqs_b, wqs_d, "wqs_b"),
                    (wk_b, wk_d, "wk_b"), (wv_b, wv_d, "wv_b"), (w_out_b, w_out_d, "w_out_b"), (w_up_b, w_up_d, "w_up_b"),
                    (w_down_b, w_down_d, "w_down_b")]
        def do_casts(lst, pin=False):
            for dst, src, nm in lst:
                rows = dst.shape[0]
                step = 256
                for r0 in range(0, rows, step):
                    r1 = min(rows, r0 + step)
                    sc.dma('pool', dst[r0:r1, :], src[r0:r1, :], [], [D_w[nm]], D_w[nm])
                if pin:
                    sc.pinned.add(id(sc.dsem[(id(D_w[nm]), 'pool')]))

        do_casts(castlist[0:7])

        NR = NB + 1
        with contextlib.ExitStack() as st:
            ccs = sb(st, "ccs", [NR, D], F32)
            sig = sb(st, "sig", [NR, D], F32)
            scT = sb(st, "scT", [128, 8, NR], F32)
            bmod = sb(st, "bmod", [NR, 6 * D], F32)
            modsb = sb(st, "modsb", [NR, 6 * D], F32)
            wm = [sb(st, "wm%d" % i, [128, 8, 512], F32) for i in range(2)]
            pT0 = ps(st, "pT0", [128, 8, NR], F32)
            pm = [ps(st, "pm%d" % i, [NR, 512], F32) for i in range(2)]
            sc.dma('sp', ccs[:], cc_d, [], [ccs], ccs)
            sc.dma('sp', bmod[:], b_mod_d.partition_broadcast(NR), [], [bmod], bmod)
            act(lambda e: e.activation(out=sig[:], in_=ccs[:], func=AF.Exp, scale=-1.0), [ccs], [sig])
            dve(lambda e: e.tensor_scalar(out=sig[:], in0=sig[:], scalar1=1.0, scalar2=None, op0=ALU.add), [sig], [sig])
            dve(lambda e: e.reciprocal(out=sig[:], in_=sig[:]), [sig], [sig])
            dve(lambda e: e.tensor_tensor(out=ccs[:], in0=ccs[:], in1=sig[:], op=ALU.mult), [ccs, sig], [ccs])
            for k in range(8):
                tr(pT0[:, k, :], ccs[:, k * 128:(k + 1) * 128], identf[0:NR, 0:NR], [ccs, identf], [pT0])
            dve(lambda e: e.tensor_copy(out=scT[:], in_=pT0[:]), [pT0], [scT])
            for n in range(12):
                w = wm[n % 2]
                sc.dma('sp', w[:], w_mod_d[:, n * 512:(n + 1) * 512].rearrange("(k p) c -> p k c", p=128), [], [w], w)
                p = pm[n % 2]
                for k in range(8):
                    mm(p[:], scT[:, k, :], w[:, k, :], k == 0, k == 7, [scT, w], [p])
                dve(lambda e, p=p, n=n: e.tensor_tensor(out=modsb[:, n * 512:(n + 1) * 512], in0=p[:], in1=bmod[:, n * 512:(n + 1) * 512], op=ALU.add),
                    [p, bmod], [modsb])
            sc.dma('sp', modv_d, modsb[:], [modsb], [D_modv], modsb)
            sc.barrier()

        for b in range(NB):
            ms = contextlib.ExitStack()
            es.enter_context(ms)
            modT = [sb(ms, "modT%d" % i, [128, 48], F32) for i in range(2)]
            G1 = [sb(ms, "G1_%d" % i, [128, 8], F32) for i in range(2)]
            G4 = sb(ms, "G4", [128, 8], F32)
            bs = contextlib.ExitStack()
            es.enter_context(bs)
            cqnT = sb(bs, "cqnT", [128, 3, S], BF16)
            ckvnT = sb(bs, "ckvnT", [128, 2, L], BF16)
            krT = sb(bs, "krT", [96, L], BF16)
            dtv = sb(bs, "dtv", [128, NCH, 32], F32)
            xstk = contextlib.ExitStack()
            es.enter_context(xstk)
            xbcT = [sb(xstk, "xbcT%d" % j, [128, XW], BF16) for j in range(12)]

            with contextlib.ExitStack() as st:
                mt = sb(st, "mt", [48, 2, 128], F32)
                pmt = ps(st, "pmt", [128, 2, 48], F32)
                for i, r in enumerate([b, NB]):
                    sc.dma('sp', mt[:, i, :], modv_d[r:r + 1, :].rearrange("o (c p) -> (o c) p", p=128), [D_modv], [mt], mt)
                for i in range(2):
                    tr(pmt[:, i, :], mt[:, i, :], identf[0:48, 0:48], [mt, identf], [pmt])
                    dve(lambda e, i=i: e.tensor_copy(out=modT[i][:], in_=pmt[:, i, :]), [pmt], [modT[i]])
                    dve(lambda e, i=i: e.scalar_tensor_tensor(out=G1[i][:], in0=modT[i][:, 8:16], scalar=1.0, in1=colv[:, R_MIXPRE:R_MIXPRE + 8], op0=ALU.add, op1=ALU.mult),
                        [modT[i], colv], [G1[i]])
                dve(lambda e: e.scalar_tensor_tensor(out=G4[:], in0=modT[0][:, 32:40], scalar=1.0, in1=colv[:, R_FPRE:R_FPRE + 8], op0=ALU.add, op1=ALU.mult),
                    [modT[0], colv], [G4])
                sc.barrier()

            with contextlib.ExitStack() as st:
                w_in_sb = sb(st, "w_in_sb", [128, 8, INW], BF16)
                wkr_sb = sb(st, "wkr_sb", [128, 8, 192], BF16)
                c1 = sb(st, "c1", [96, L], BF16)
                c2 = sb(st, "c2", [96, L], BF16)
                dtb = smallb
                xt = [sb(st, "xt%d" % i, [128, D], F32) for i in range(2)]
                xs = [sb(st, "xs%d" % i, [128, D], BF16) for i in range(2)]
                junk = sb(st, "junk", [128, D], BF16)
                junk2 = sb(st, "junk2", [128, 640], BF16)
                ssb = [sb(st, "ssb%d" % i, [128, 2], F32) for i in range(2)]
                ssq = [sb(st, "ssq%d" % i, [128, 2], F32) for i in range(2)]
                sskv = [sb(st, "sskv%d" % i, [128, 2], F32) for i in range(2)]
                hT = [sb(st, "hT%d" % i, [128, 8, 512], BF16) for i in range(2)]
                hTk = [[Buf(hT[i].t, "hT%d_%d" % (i, k)) for k in range(8)] for i in range(2)]
                qkv = [sb(st, "qkv%d" % i, [128, 640], BF16) for i in range(2)]
                zsb = [sb(st, "zsb%d" % i, [128, D], BF16) for i in range(2)]
                dtt = sb(st, "dtt", [128, 32], F32)
                pAs = [sb(st, "pAs%d" % i, [128, 672], F32) for i in range(2)]
                tmpT = [sb(st, "tmpT%d" % i, [128, 8, 128], BF16) for i in range(2)]
                kra = sb(st, "kra", [96, 512], F32)
                krb = sb(st, "krb", [96, 512], F32)
                pT = ps(st, "pT", [128, 8, 128], BF16)
                pT2 = ps(st, "pT2", [128, 8, 128], BF16)
                pA = ps(st, "pA", [128, 1024], F32)
                pZ = ps(st, "pZ", [128, 1024], F32)
                pF = [ps(st, "pF%d" % i, [128, 512], F32) for i in range(2)]
                cqk = [Buf(cqnT.t, "cqnT%d" % k) for k in range(3)]
                ckvk = [Buf(ckvnT.t, "ckvnT%d" % k) for k in range(2)]
                sc.dma('sp', w_in_sb[:], w_in_b.rearrange("(k p) c -> p k c", p=128), [D_w["w_in_b"]], [w_in_sb], w_in_sb)
                sc.dma('sp', wkr_sb[:], wkr_b.rearrange("(k p) c -> p k c", p=128), [D_w["wkr_b"]], [wkr_sb], wkr_sb)
                sc.dma('pool', c1[:], c1_d, [], [c1], c1)
                sc.dma('pool', c2[:], c2_d, [], [c2], c2)
                for j in range(12):
                    pool(lambda e, j=j: e.memset(xbcT[j][:], 0.0), [], [xbcT[j]])
                blocks = [(0, CTX, True)] + [(CTX + i * 512, 512, False) for i in range(S // 512)]
                tiles = []
                for bi, (t0, T, isctx) in enumerate(blocks):
                    for tt in range(T // 128):
                        tiles.append((bi, t0, T, isctx, tt, tt == T // 128 - 1))
                fcount = [0]

                def p1_front(ti):
                    bi, t0, T, isctx, tt, lastt = tiles[ti]
                    h = hT[bi % 2]; hk = hTk[bi % 2]
                    mi = 1 if isctx else 0
                    g0 = t0 + tt * 128
                    x_ = xt[ti % 2]; xs_ = xs[ti % 2]; ss_ = ssb[ti % 2]
                    src = ctx_d[b, g0:g0 + 128, :] if isctx else x_d[b, g0 - CTX:g0 - CTX + 128, :]
                    sc.dma('sp', x_[:], src, [], [x_], x_)
                    act(lambda e: e.activation(out=junk[:], in_=x_[:], func=AF.Square, accum_out=ss_[:, 0:1]), [x_], [junk, ss_])
                    rstd_from_ss(ss_, D)
                    dve(lambda e: e.tensor_scalar(out=xs_[:], in0=x_[:], scalar1=ss_[:, 1:2], scalar2=None, op0=ALU.mult), [x_, ss_], [xs_])
                    for k in range(8):
                        tr(pT[:, k, :], xs_[:, k * 128:(k + 1) * 128], identb[:], [xs_, identb], [pT], sig=(k == 7))
                    tm_ = tmpT[ti % 2]
                    dve(lambda e: e.tensor_tensor(out=tm_[:], in0=pT[:], in1=G1[mi][:].unsqueeze(2).to_broadcast([128, 8, 128]), op=ALU.mult), [pT, G1[mi]], [tm_])
                    pool(lambda e: e.tensor_tensor(out=h[:, :, tt * 128:(tt + 1) * 128], in0=tm_[:], in1=modT[mi][:, 0:8].unsqueeze(2).to_broadcast([128, 8, 128]), op=ALU.add),
                         [tm_, modT[mi]], hk)

                def p1_back(ti):
                    bi, t0, T, isctx, tt, lastt = tiles[ti]
                    h = hT[bi % 2]; hk = hTk[bi % 2]
                    g0 = t0 + tt * 128
                    ch = g0 // 128
                    hs = lambda k: h[:, k, tt * 128:(tt + 1) * 128]
                    for (c0, c1_, dst0) in [(0, 512, 0), (512, 640, 512)]:
                        for k in range(8):
                            mm(pA[:, dst0:dst0 + (c1_ - c0)], hs(k), w_in_sb[:, k, c0:c1_], k == 0, k == 7, [hk[k], w_in_sb], [pA])
                    for k in range(8):
                        mm(pA[:, 640:672], hs(k), w_in_sb[:, k, 3232:3264], k == 0, k == 7, [hk[k], w_in_sb], [pA])
                    pAs_ = pAs[ti % 2]
                    act(lambda e: e.activation(out=pAs_[:], in_=pA[:, 0:672], func=AF.Copy), [pA], [pAs_])
                    if not isctx:
                        for (c0, dst0) in [(672, 0), (1184, 512)]:
                            for k in range(8):
                                mm(pZ[:, dst0:dst0 + 512], hs(k), w_in_sb[:, k, c0:c0 + 512], k == 0, k == 7, [hk[k], w_in_sb], [pZ])
                        z_ = zsb[ti % 2]
                        act(lambda e: e.activation(out=z_[:], in_=pZ[:], func=AF.Silu), [pZ], [z_])
                        sc.dma('sp', zs_d[g0 - CTX:g0 - CTX + 128, :], z_[:], [z_], [D_zs], z_)
                    if lastt:
                        p1_blockmm(ti)

                def p1_back_b(ti):
                    bi, t0, T, isctx, tt, lastt = tiles[ti]
                    g0 = t0 + tt * 128
                    ch = g0 // 128
                    pAs_ = pAs[ti % 2]
                    dve(lambda e: e.tensor_tensor(out=dtt[:], in0=pAs_[:, 640:672], in1=dtb[:, 32:64], op=ALU.add), [pAs_, dtb], [dtt])
                    act(lambda e: e.activation(out=dtt[:], in_=dtt[:], func=AF.Exp), [dtt], [dtt])
                    act(lambda e: e.activation(out=dtv[:, ch, :], in_=dtt[:], func=AF.Ln, bias=1.0), [dtt], [dtv])
                    q_ = qkv[ti % 2]; sq_ = ssq[ti % 2]; skv_ = sskv[ti % 2]
                    act(lambda e: e.activation(out=junk2[:, 0:384], in_=pAs_[:, 0:384], func=AF.Square, accum_out=sq_[:, 0:1]), [pAs_], [junk2, sq_])
                    act(lambda e: e.activation(out=junk2[:, 384:640], in_=pAs_[:, 384:640], func=AF.Square, accum_out=skv_[:, 0:1]), [pAs_], [junk2, skv_])
                    rstd_from_ss(sq_, 384)
                    rstd_from_ss(skv_, 256)
                    dve(lambda e: e.tensor_scalar(out=q_[:, 0:384], in0=pAs_[:, 0:384], scalar1=sq_[:, 1:2], scalar2=None, op0=ALU.mult), [pAs_, sq_], [q_])
                    dve(lambda e: e.tensor_scalar(out=q_[:, 384:640], in0=pAs_[:, 384:640], scalar1=skv_[:, 1:2], scalar2=None, op0=ALU.mult), [pAs_, skv_], [q_])

                def p1_back_tr(ti):
                    bi, t0, T, isctx, tt, lastt = tiles[ti]
                    g0 = t0 + tt * 128
                    q_ = qkv[ti % 2]
                    for k in range(5):
                        tr(pT2[:, k, :], q_[:, k * 128:(k + 1) * 128], identb[:], [q_, identb], [pT2], sig=(k == 4))
                    if not isctx:
                        dve(lambda e: e.tensor_tensor(out=cqnT[:, :, g0 - CTX:g0 - CTX + 128], in0=pT2[:, 0:3, :],
                                                      in1=colv[:, R_QN:R_QN + 3].unsqueeze(2).to_broadcast([128, 3, 128]), op=ALU.mult), [pT2, colv], cqk)
                    dve(lambda e: e.tensor_tensor(out=ckvnT[:, :, g0:g0 + 128], in0=pT2[:, 3:5, :],
                                                  in1=colv[:, R_KVN:R_KVN + 2].unsqueeze(2).to_broadcast([128, 2, 128]), op=ALU.mult), [pT2, colv], ckvk)

                def p1_blockmm(ti):
                    bi, t0, T, isctx, tt, lastt = tiles[ti]
                    h = hT[bi % 2]; hk = hTk[bi % 2]
                    if True:
                        xoff = (XOFF_C + t0) if isctx else (XOFF_L + t0 - CTX)
                        for j in range(12):
                            p = pF[fcount[0] % 2]; fcount[0] += 1
                            for k in range(8):
                                mm(p[:, 0:T], w_in_sb[:, k, 1696 + j * 128:1696 + (j + 1) * 128], h[:, k, 0:T], k == 0, k == 7, [hk[k], w_in_sb], [p])
                            if j % 2 == 0:
                                act(lambda e, p=p, j=j: e.activation(out=xbcT[j][:, xoff:xoff + T], in_=p[:, 0:T], func=AF.Copy), [p], [xbcT[j]])
                            else:
                                dve(lambda e, p=p, j=j: e.tensor_copy(out=xbcT[j][:, xoff:xoff + T], in_=p[:, 0:T]), [p], [xbcT[j]])
                        pa = pF[fcount[0] % 2]; fcount[0] += 1
                        pb = pF[fcount[0] % 2]; fcount[0] += 1
                        for k in range(8):
                            mm(pa[0:96, 0:T], wkr_sb[:, k, 0:96], h[:, k, 0:T], k == 0, k == 7, [hk[k], wkr_sb], [pa])
                        for k in range(8):
                            mm(pb[0:96, 0:T], wkr_sb[:, k, 96:192], h[:, k, 0:T], k == 0, k == 7, [hk[k], wkr_sb], [pb])
                        dve(lambda e: e.tensor_tensor(out=kra[:, 0:T], in0=pa[0:96, 0:T], in1=c1[:, t0:t0 + T], op=ALU.mult), [pa, c1], [kra])
                        dve(lambda e: e.tensor_tensor(out=krb[:, 0:T], in0=pb[0:96, 0:T], in1=c2[:, t0:t0 + T], op=ALU.mult), [pb, c2], [krb])
                        pool(lambda e: e.tensor_tensor(out=krT[:, t0:t0 + T], in0=kra[:, 0:T], in1=krb[:, 0:T], op=ALU.add), [kra, krb], [krT])

                p1_front(0)
                nt_ = len(tiles)
                for ti in range(nt_):
                    if ti + 1 < nt_:
                        p1_front(ti + 1)
                    p1_back(ti)
                    if ti >= 1:
                        p1_back_b(ti - 1)
                    if ti >= 2:
                        p1_back_tr(ti - 2)
                p1_back_b(nt_ - 1)
                if nt_ >= 2:
                    p1_back_tr(nt_ - 2)
                p1_back_tr(nt_ - 1)
                sc.barrier()

            if b == 0:
                do_casts(castlist[7:9], pin=True)
            with contextlib.ExitStack() as st:
                dW = sb(st, "dW", [128, 60, 128], BF16)
                cto = [sb(st, "cto%d" % i, [128, L], BF16) for i in range(2)]
                pcv = [ps(st, "pcv%d" % i, [128, 512], F32) for i in range(4)]
                for kk in range(5):
                    for j in range(12):
                        r = kk * 12 + j
                        dve(lambda e, r=r: e.tensor_scalar(out=dW[:, r, :], in0=identf[:], scalar1=colv[:, R_CW + r:R_CW + r + 1], scalar2=None, op0=ALU.mult), [identf, colv], [dW])
                pi = 0
                cblocks = [(XOFF_C, 0, CTX)] + [(XOFF_L + i * 512, CTX + i * 512, 512) for i in range(S // 512)]
                for j in range(12):
                    t_ = cto[j % 2]
                    xb_ = xbcT[j]
                    for (off, o0, n) in cblocks:
                        p = pcv[pi % 4]; pi += 1
                        for kk in range(5):
                            mm(p[:, 0:n], dW[:, kk * 12 + j, :], xb_[:, off + kk - 2:off + kk - 2 + n], kk == 0, kk == 4, [dW, xb_], [p])
                        act(lambda e, p=p, o0=o0, n=n, j=j: e.activation(out=t_[:, o0:o0 + n], in_=p[:, 0:n], func=AF.Silu, bias=colv[:, R_CB + j:R_CB + j + 1]), [p, colv], [t_])
                    dve(lambda e, t_=t_, xb_=xb_: e.tensor_copy(out=xb_[:, XOFF_C:XOFF_C + CTX], in_=t_[:, 0:CTX]), [t_], [xb_])
                    dve(lambda e, t_=t_, xb_=xb_: e.tensor_copy(out=xb_[:, XOFF_L:XOFF_L + S], in_=t_[:, CTX:L]), [t_], [xb_])
                sc.barrier()

            def xcol(ch):
                return XOFF_C + ch * 128 if ch < NCX else XOFF_L + (ch - NCX) * 128

            with contextlib.ExitStack() as st:
                NS = NCH * 32
                a_all = sb(st, "a_all", [128, NCH, 32], F32)
                acs_all = sb(st, "acs_all", [128, NCH, 32], F32)
                coef_all = sb(st, "coef_all", [128, NCH, 32], F32)
                dec_all = sb(st, "dec_all", [128, NCH, 32], F32)
                ea_all = sb(st, "ea_all", [128, NCH, 32], F32)
                Sf = sb(st, "Sf", [128, D], F32)
                Sbk = sb(st, "Sbk", [128, D], F32)
                Sfb = sb(st, "Sfb", [128, D], BF16)
                Sbt = [sb(st, "Sbt%d" % i, [128, D], BF16) for i in range(2)]
                Sst = [sb(st, "Sst%d" % i, [128, D], BF16) for i in range(2)]
                xtm = [sb(st, "xtm%d" % i, [128, D], BF16) for i in range(2)]
                btm = [sb(st, "btm%d" % i, [128, 256], BF16) for i in range(2)]
                xd = [[sb(st, "xd%d_%d" % (d, i), [128, D], BF16) for i in range(2)] for d in range(2)]
                xdw = [sb(st, "xdw%d" % i, [128, D], BF16) for i in range(2)]
                AUh = [sb(st, "AUh%d" % i, [128, 16, 128], BF16) for i in range(2)]
                AUl = [sb(st, "AUl%d" % i, [128, 16, 128], BF16) for i in range(2)]
                a_hi = sb(st, "a_hi", [128, NCH, 32], BF16)
                a_lo = sb(st, "a_lo", [128, NCH, 32], BF16)
                negU = [sb(st, "negU%d" % d, [128, 128], BF16) for d in range(2)]
                onesb = sb(st, "onesb", [128, 128], BF16)
                Eb = [sb(st, "Eb%d" % d, [128, 16, 128], BF16) for d in range(2)]
                Gs = [sb(st, "Gs%d" % i, [128, 2, 128], BF16) for i in range(2)]
                MT = [[sb(st, "MT%d_%d" % (d, i), [128, 16, 128], BF16) for i in range(2)] for d in range(2)]
                t1 = sb(st, "t1", [128, D], F32)
                yg = sb(st, "yg", [128, D], F32)
                ygb = sb(st, "ygb", [128, D], BF16)
                junk3 = sb(st, "junk3", [128, D], BF16)
                zt = [sb(st, "zt%d" % i, [128, D], BF16) for i in range(2)]
                snT = [sb(st, "snT%d" % i, [128, 8, 128], BF16) for i in range(2)]
                ss3 = [sb(st, "ss3_%d" % i, [128, 2], F32) for i in range(2)]
                negb = [sb(st, "negb%d" % d, [128, 128], BF16) for d in range(2)]
                pX = ps(st, "pX", [128, 1024], BF16)
                pRb = [ps(st, "pRb%d" % i, [128, 512], F32) for i in range(2)]
                pY = ps(st, "pY", [128, 1024], F32)
                pO = [ps(st, "pO3_%d" % i, [128, 512], F32) for i in range(3)]
                mask = {0: uf, 1: ub}
                ocnt = [0]
                rcnt = [0]

                def nextO():
                    ocnt[0] += 1
                    return pO[ocnt[0] % 3]

                for d in range(2):
                    dve(lambda e, d=d: e.tensor_scalar(out=negb[d][:], in0=mask[d][:], scalar1=-1.0, scalar2=30000.0, op0=ALU.add, op1=ALU.mult), [mask[d]], [negb[d]])
                dve(lambda e: e.tensor_tensor(out=a_all[:], in0=dtv[:], in1=smallb[:, 0:32].unsqueeze(1).to_broadcast([128, NCH, 32]), op=ALU.mult), [dtv, smallb], [a_all])
                for d in range(2):
                    dsl = slice(16 * d, 16 * d + 16)
                    p1_ = nextO(); p2_ = nextO()
                    mm(p1_[:, 0:NCH * 16], mask[d][:], a_all[:, :, dsl], True, True, [mask[d], a_all], [p1_])
                    mm(p2_[:, 0:NCH * 16], onesf[:], a_all[:, :, dsl], True, True, [onesf, a_all], [p2_])
                    dve(lambda e, p1_=p1_, dsl=dsl: e.tensor_copy(out=acs_all[:, :, dsl], in_=p1_[:, 0:NCH * 16].rearrange("p (c h) -> p c h", h=16)), [p1_], [acs_all])
                    dve(lambda e, p2_=p2_, dsl=dsl: e.tensor_tensor(out=coef_all[:, :, dsl], in0=p2_[:, 0:NCH * 16].rearrange("p (c h) -> p c h", h=16), in1=acs_all[:, :, dsl], op=ALU.subtract),
                        [p2_, acs_all], [coef_all])
                    act(lambda e, p2_=p2_, dsl=dsl: e.activation(out=dec_all[:, :, dsl], in_=p2_[:, 0:NCH * 16].rearrange("p (c h) -> p c h", h=16), func=AF.Exp), [p2_], [dec_all])
                act(lambda e: e.activation(out=coef_all[:], in_=coef_all[:], func=AF.Exp), [coef_all], [coef_all])
                act(lambda e: e.activation(out=ea_all[:], in_=acs_all[:], func=AF.Exp), [acs_all], [ea_all])
                dve(lambda e: e.tensor_copy(out=a_hi[:], in_=a_all[:]), [a_all], [a_hi])
                dve(lambda e: e.tensor_tensor(out=a_lo[:], in0=a_all[:], in1=a_hi[:], op=ALU.subtract), [a_all, a_hi], [a_lo])
                dve(lambda e: e.memset(onesb[:], 1.0), [], [onesb])
                for d in range(2):
                    dve(lambda e, d=d: e.tensor_scalar(out=negU[d][:], in0=mask[d][:], scalar1=-1.0, scalar2=None, op0=ALU.mult), [mask[d]], [negU[d]])
                dve(lambda e: e.tensor_tensor(out=coef_all[:], in0=coef_all[:], in1=dtv[:], op=ALU.mult), [coef_all, dtv], [coef_all])

                def h3(ap_):
                    return ap_.rearrange("p (h q) -> p h q", h=16)

                def bc64(ap_):
                    return ap_.unsqueeze(2).to_broadcast([128, 16, 64])

                def tokmajor(ch, slot):
                    c0 = xcol(ch)
                    x_ = xtm[slot]; b_ = btm[slot]
                    for j in range(8):
                        tr(pX[:, j * 128:(j + 1) * 128], xbcT[j][:, c0:c0 + 128], identb[:], [xbcT[j], identb], [pX], sig=(j == 7))
                    act(lambda e: e.activation(out=x_[:], in_=pX[:], func=AF.Copy), [pX], [x_])
                    for j in range(2):
                        tr(pX[:, j * 128:(j + 1) * 128], xbcT[8 + j][:, c0:c0 + 128], identb[:], [xbcT[8 + j], identb], [pX], sig=(j == 1))
                    act(lambda e: e.activation(out=b_[:], in_=pX[:, 0:256], func=AF.Copy), [pX], [b_])

                bkc = [0]
                pBk = pO + pRb

                def nextBk():
                    bkc[0] += 1
                    return pBk[bkc[0] % 5]

                def state_A(ch, d, slot, nextO=nextO):
                    tokmajor(ch, slot)
                    x_ = xtm[slot]; b_ = btm[slot]; w_ = xdw[slot]
                    pool(lambda e: e.tensor_tensor(out=h3(w_[:]), in0=h3(x_[:]), in1=bc64(coef_all[:, ch, 16 * d:16 * d + 16]), op=ALU.mult), [x_, coef_all], [w_])
                    banks = []
                    for g in range(2):
                        p = nextO()
                        mm(p[:], b_[:, g * 128:(g + 1) * 128], w_[:, g * 512:(g + 1) * 512], True, True, [b_, w_], [p])
                        banks.append(p)
                    return banks

                def state_B(ch, d, Sst_, banks):
                    dve(lambda e: e.tensor_tensor(out=h3(Sst_[:]), in0=h3(Sst_[:]), in1=bc64(dec_all[:, ch, 16 * d:16 * d + 16]), op=ALU.mult), [Sst_, dec_all], [Sst_])
                    for g in range(2):
                        dve(lambda e, g=g: e.tensor_tensor(out=Sst_[:, g * 512:(g + 1) * 512], in0=Sst_[:, g * 512:(g + 1) * 512], in1=banks[g][:], op=ALU.add), [Sst_, banks[g]], [Sst_])

                D_sb = dbuf(None, "sbst")
                skc = [0]

                def store_Sb(lc):
                    s_ = Sst[skc[0] % 2]; skc[0] += 1
                    act(lambda e: e.activation(out=s_[:], in_=Sbk[:], func=AF.Copy), [Sbk], [s_])
                    sc.dma('sp', sbst_d[lc], s_[:], [s_], [D_sb], s_)

                dve(lambda e: e.memset(Sbk[:], 0.0), [], [Sbk])
                dve(lambda e: e.memset(Sf[:], 0.0), [], [Sf])
                order_b = list(range(NCX - 1, -1, -1)) + list(range(NCH - 1, NCX, -1))
                prev = None
                for i, ch in enumerate(order_b):
                    banks = state_A(ch, 1, i % 2, nextBk)
                    if prev is not None:
                        pch, pbanks = prev
                        if pch >= NCX:
                            store_Sb(pch - NCX)
                        state_B(pch, 1, Sbk, pbanks)
                    prev = (ch, banks)
                pch, pbanks = prev
                if pch >= NCX:
                    store_Sb(pch - NCX)
                state_B(pch, 1, Sbk, pbanks)
                store_Sb(0)
                sc.barrier(release=False)
                for i, ch in enumerate(range(NCX)):
                    banks = state_A(ch, 0, i % 2)
                    state_B(ch, 0, Sf, banks)

                maskbf = {0: ufb, 1: ubb}

                def lat_AU(ch, ds=(0, 1)):
                    for d in ds:
                        dve(lambda e, d=d: e.tensor_tensor(out=AUh[d][:], in0=maskbf[d][:].unsqueeze(1).to_broadcast([128, 16, 128]),
                                                           in1=a_hi[:, ch, 16 * d:16 * d + 16].unsqueeze(2).to_broadcast([128, 16, 128]), op=ALU.mult), [maskbf[d], a_hi], [AUh[d]])
                        dve(lambda e, d=d: e.tensor_tensor(out=AUl[d][:], in0=maskbf[d][:].unsqueeze(1).to_broadcast([128, 16, 128]),
                                                           in1=a_lo[:, ch, 16 * d:16 * d + 16].unsqueeze(2).to_broadcast([128, 16, 128]), op=ALU.mult), [maskbf[d], a_lo], [AUl[d]])

                def lat_A(ch, slot):
                    lc = ch - NCX
                    c0 = xcol(ch)
                    x_ = xtm[slot]; w_ = xdw[slot]
                    tokmajor(ch, slot)
                    sc.dma('sp', zt[slot][:], zs_d[lc * 128:(lc + 1) * 128, :], [D_zs], [zt[slot]], zt[slot])
                    sc.dma('sp', Sbt[slot][:], sbst_d[lc], [D_sb], [Sbt[slot]], Sbt[slot])
                    for d in range(2):
                        pool(lambda e, d=d: e.tensor_tensor(out=h3(xd[d][slot][:]), in0=h3(x_[:]), in1=bc64(dtv[:, ch, 16 * d:16 * d + 16]), op=ALU.mult), [x_, dtv], [xd[d][slot]])
                    pool(lambda e: e.tensor_tensor(out=h3(w_[:]), in0=h3(x_[:]), in1=bc64(coef_all[:, ch, 0:16]), op=ALU.mult), [x_, coef_all], [w_])
                    pg = nextO()
                    for g in range(2):
                        mm(pg[:, g * 128:(g + 1) * 128], xbcT[8 + g][:, c0:c0 + 128], xbcT[10 + g][:, c0:c0 + 128], True, True, [xbcT[8 + g], xbcT[10 + g]], [pg], sig=(g == 1))
                    act(lambda e: e.activation(out=Gs[slot][:], in_=pg[:, 0:256].rearrange("p (g l) -> p g l", g=2), func=AF.Copy), [pg], [Gs[slot]])
                    for d in range(2):
                        for r4 in range(4):
                            pr = pRb[rcnt[0] % 2]; rcnt[0] += 1
                            hsl = slice(16 * d + r4 * 4, 16 * d + r4 * 4 + 4)
                            mm(pr[:], onesb[:], AUh[d][:, r4 * 4:(r4 + 1) * 4, :].rearrange("p h l -> p (h l)"), True, False, [onesb, AUh[d]], [pr], sig=False)
                            mm(pr[:], onesb[:], AUl[d][:, r4 * 4:(r4 + 1) * 4, :].rearrange("p h l -> p (h l)"), False, False, [onesb, AUl[d]], [pr], sig=False)
                            mm(pr[:], negU[d][:], a_hi[:, ch, hsl].unsqueeze(2).to_broadcast([128, 4, 128]), False, False, [negU[d], a_hi], [pr], sig=False)
                            mm(pr[:], negU[d][:], a_lo[:, ch, hsl].unsqueeze(2).to_broadcast([128, 4, 128]), False, False, [negU[d], a_lo], [pr], sig=False)
                            mm(pr[:], identb[:], negb[d][:].unsqueeze(1).to_broadcast([128, 4, 128]), False, True, [identb, negb[d]], [pr], sig=True)
                            act(lambda e, d=d, r4=r4, pr=pr: e.activation(out=Eb[d][:, r4 * 4:(r4 + 1) * 4, :], in_=pr[:].rearrange("p (h l) -> p h l", h=4), func=AF.Exp), [pr], [Eb[d]])
                        en = 'dve'
                        for g in range(2):
                            sc.op(en, lambda e, d=d, g=g: e.tensor_tensor(out=MT[d][slot][:, g * 8:(g + 1) * 8, :], in0=Eb[d][:, g * 8:(g + 1) * 8, :],
                                                                        in1=Gs[slot][:, g:g + 1, :].to_broadcast([128, 8, 128]), op=ALU.mult), [Eb[d], Gs[slot]], [MT[d][slot]])

                def lat_B(ch, slot):
                    lc = ch - NCX
                    c0 = xcol(ch)
                    b_ = btm[slot]; w_ = xdw[slot]; s3 = ss3[slot]
                    act(lambda e: e.activation(out=Sfb[:], in_=Sf[:], func=AF.Copy), [Sf], [Sfb])
                    for j in range(8):
                        mm(pY[:, j * 128:(j + 1) * 128], xbcT[j][:, c0:c0 + 128], diagD[:, j, :], j % 4 == 0, False, [xbcT[j], diagD], [pY], sig=False)
                    for d in range(2):
                        for hh in range(16):
                            last = (d == 1)
                            mm(pY[:, hh * 64:(hh + 1) * 64], MT[d][slot][:, hh, :], xd[d][slot][:, hh * 64:(hh + 1) * 64], False, last and hh % 8 == 7,
                               [MT[d][slot], xd[d][slot]], [pY], sig=(last and hh == 15))
                    for d in range(2):
                        srcS = Sfb if d == 0 else Sbt[slot]
                        dst = t1 if d == 0 else yg
                        for g in range(2):
                            p = nextO()
                            mm(p[:], xbcT[10 + g][:, c0:c0 + 128], srcS[:, g * 512:(g + 1) * 512], True, True, [xbcT[10 + g], srcS], [p])
                            dve(lambda e, p=p, g=g, d=d, dst=dst: e.tensor_tensor(out=dst[:, g * 512:(g + 1) * 512].rearrange("p (h q) -> p h q", h=8),
                                                                                 in0=p[:].rearrange("p (h q) -> p h q", h=8),
                                                                                 in1=ea_all[:, ch, 16 * d + 8 * g:16 * d + 8 * g + 8].unsqueeze(2).to_broadcast([128, 8, 64]), op=ALU.mult),
                                [p, ea_all], [dst])
                    if ch < NCH - 1:
                        banks = []
                        for g in range(2):
                            p = nextO()
                            mm(p[:], b_[:, g * 128:(g + 1) * 128], w_[:, g * 512:(g + 1) * 512], True, True, [b_, w_], [p])
                            banks.append(p)
                        state_B(ch, 0, Sf, banks)

                def lat_B1b(ch, slot):
                    s3 = ss3[slot]
                    pool(lambda e: e.tensor_tensor(out=t1[:], in0=t1[:], in1=yg[:], op=ALU.add), [t1, yg], [t1])
                    dve(lambda e: e.tensor_tensor(out=yg[:], in0=pY[:], in1=t1[:], op=ALU.add), [pY, t1], [yg])

                def lat_B1c(ch, slot):
                    s3 = ss3[slot]
                    pool(lambda e: e.tensor_tensor(out=yg[:], in0=yg[:], in1=zt[slot][:], op=ALU.mult), [yg, zt[slot]], [yg])
                    act(lambda e: e.activation(out=junk3[:], in_=yg[:], func=AF.Square, accum_out=s3[:, 0:1]), [yg], [junk3, s3])
                    rstd_from_ss(s3, D)
                    act(lambda e: e.activation(out=ygb[:], in_=yg[:], func=AF.Copy, scale=s3[:, 1:2]), [yg, s3], [ygb])

                def lat_B2(ch, slot):
                    lc = ch - NCX
                    for j in range(8):
                        tr(pX[:, j * 128:(j + 1) * 128], ygb[:, j * 128:(j + 1) * 128], identb[:], [ygb, identb], [pX], sig=(j == 7))
                    dve(lambda e: e.tensor_tensor(out=snT[slot][:], in0=pX[:].rearrange("p (j t) -> p j t", j=8),
                                                  in1=colv[:, R_SN:R_SN + 8].unsqueeze(2).to_broadcast([128, 8, 128]), op=ALU.mult), [pX, colv], [snT[slot]])
                    sc.dma('sp', ssdnT_d[lc], snT[slot][:].rearrange("p j t -> p (j t)"), [snT[slot]], [D_ssdnT], snT[slot])

                lat_AU(NCX)
                lat_A(NCX, 0)
                for i, ch in enumerate(range(NCX, NCH)):
                    lat_B(ch, i % 2)
                    lat_B1b(ch, i % 2)
                    if ch + 1 < NCH:
                        lat_AU(ch + 1, (0,))
                    if i >= 1:
                        lat_B2(ch - 1, (i - 1) % 2)
                    if ch + 1 < NCH:
                        lat_AU(ch + 1, (1,))
                        lat_A(ch + 1, (i + 1) % 2)
                    lat_B1c(ch, i % 2)
                lat_B2(NCH - 1, (NCL - 1) % 2)
                sc.barrier()
            xstk.close()

            with contextlib.ExitStack() as st:
                wq_sb = sb(st, "wq_sb", [128, 3, NH * 96], BF16)
                wqs_sb = sb(st, "wqs_sb", [128, 3, NH * 96], BF16)
                wk_sb = sb(st, "wk_sb", [128, 2, NH * 64], BF16)
                wv_sb = sb(st, "wv_sb", [128, 2, NH * 64], BF16)
                c1 = sb(st, "c1a", [96, S], BF16)
                c2 = sb(st, "c2a", [96, S], BF16)
                Vaug = sb(st, "Vaug", [128, NCH, NH, 128], BF16)
                qT = [sb(st, "qT%d" % i, [96, S], BF16) for i in range(2)]
                kT = [sb(st, "kT%d" % i, [96, L], BF16) for i in range(2)]
                qa = sb(st, "qa", [96, 512], F32)
                qb_ = sb(st, "qb_", [96, 512], F32)
                NPT = 4
                PT = [sb(st, "PT%d" % i, [128, 1024], BF16) for i in range(NPT)]
                rc = sb(st, "rc", [64, 512], F32)
                ao = [sb(st, "ao%d" % i, [64, S], BF16) for i in range(2)]
                pSs = [ps(st, "pSs%d" % i, [128, 1024], F32) for i in range(2)]
                pP = ps(st, "pP", [128, 1024], F32)
                pO = [ps(st, "pO%d" % i, [128, 512], F32) for i in range(2)]
                sc.dma('sp', wq_sb[:], wq_b.rearrange("(k p) c -> p k c", p=128), [D_w["wq_b"]], [wq_sb], wq_sb)
                sc.dma('sp', wqs_sb[:], wqs_b.rearrange("(k p) c -> p k c", p=128), [D_w["wqs_b"]], [wqs_sb], wqs_sb)
                sc.dma('sp', wk_sb[:], wk_b.rearrange("(k p) c -> p k c", p=128), [D_w["wk_b"]], [wk_sb], wk_sb)
                sc.dma('sp', wv_sb[:], wv_b.rearrange("(k p) c -> p k c", p=128), [D_w["wv_b"]], [wv_sb], wv_sb)
                sc.dma('pool', c1[:], c1_d[:, CTX:L], [], [c1], c1)
                sc.dma('pool', c2[:], c2_d[:, CTX:L], [], [c2], c2)
                for kt0 in range(0, NCH, 3):
                    pool(lambda e, kt0=kt0: e.memset(Vaug[:, kt0:min(NCH, kt0 + 3), :, 64:128], 1.0), [], [Vaug])
                kblocks = [(i * 512, min(512, L - i * 512)) for i in range((L + 511) // 512)]
                for kt in range(NCH):
                    for half in range(2):
                        p = pSs[half]
                        for k in range(2):
                            mm(p[:, 0:512], ckvnT[:, k, kt * 128:(kt + 1) * 128], wv_sb[:, k, half * 512:(half + 1) * 512], k == 0, k == 1, [ckvnT, wv_sb], [p])
                        dve(lambda e, p=p, kt=kt, half=half: e.tensor_copy(out=Vaug[:, kt, half * 8:(half + 1) * 8, 0:64], in_=p[:, 0:512].rearrange("p (h q) -> p h q", h=8)), [p], [Vaug])

                def prep_pieces(h):
                    q_ = qT[h % 2]; k_ = kT[h % 2]
                    pcs = []
                    for qb in range(S // 512):
                        def f(qb=qb):
                            for k in range(3):
                                mm(pP[0:96, 0:512], wq_sb[:, k, h * 96:(h + 1) * 96], cqnT[:, k, qb * 512:(qb + 1) * 512], k == 0, k == 2, [wq_sb, cqnT], [pP])
                            for k in range(3):
                                mm(pP[0:96, 512:1024], wqs_sb[:, k, h * 96:(h + 1) * 96], cqnT[:, k, qb * 512:(qb + 1) * 512], k == 0, k == 2, [wqs_sb, cqnT], [pP])
                            dve(lambda e: e.tensor_tensor(out=qa[:], in0=pP[0:96, 0:512], in1=c1[:, qb * 512:(qb + 1) * 512], op=ALU.mult), [pP, c1], [qa])
                            dve(lambda e: e.tensor_tensor(out=qb_[:], in0=pP[0:96, 512:1024], in1=c2[:, qb * 512:(qb + 1) * 512], op=ALU.mult), [pP, c2], [qb_])
                            pool(lambda e: e.tensor_tensor(out=q_[:, qb * 512:(qb + 1) * 512], in0=qa[:], in1=qb_[:], op=ALU.add), [qa, qb_], [q_])
                        pcs.append(f)
                    for (k0, kn) in kblocks:
                        def f(k0=k0, kn=kn):
                            for k in range(2):
                                mm(pP[0:64, 0:kn], wk_sb[:, k, h * 64:(h + 1) * 64], ckvnT[:, k, k0:k0 + kn], k == 0, k == 1, [wk_sb, ckvnT], [pP])
                            dve(lambda e: e.tensor_copy(out=k_[0:64, k0:k0 + kn], in_=pP[0:64, 0:kn]), [pP], [k_])
                        pcs.append(f)
                    pcs.append(lambda: pool(lambda e: e.tensor_copy(out=k_[64:96, :], in_=krT[64:96, :]), [krT], [k_]))
                    return pcs

                for f in prep_pieces(0):
                    f()
                ngr = NCH // 2
                units = [(h, qb) for h in range(NH) for qb in range(S // 512)]
                groups = [(h, qb, gr) for (h, qb) in units for gr in range(ngr)]
                pending = None
                gi = 0
                pend_prep = []

                def emit_pv(item):
                    h, qb, gr, pt_ = item
                    po = pO[(h * (S // 512) + qb) % 2]
                    ao_ = ao[h % 2]
                    for u in range(2):
                        kt = gr * 2 + u
                        mm(po[:], Vaug[:, kt, h, :], pt_[:, u * 512:(u + 1) * 512], kt == 0, kt == NCH - 1, [Vaug, pt_], [po], sig=(u == 1))
                    if gr == ngr - 1:
                        dve(lambda e: e.reciprocal(out=rc[:], in_=po[64:128, :]), [po], [rc])
                        dve(lambda e: e.tensor_tensor(out=ao_[:, qb * 512:(qb + 1) * 512], in0=po[0:64, :], in1=rc[:], op=ALU.mult), [po, rc], [ao_])
                        if qb == S // 512 - 1:
                            sc.dma('sp', attnT_d[:, (h % 2) * 64:(h % 2) * 64 + 64, (h // 2) * 128:(h // 2 + 1) * 128].rearrange("t p c -> p t c"), ao_[:].rearrange("p (t c) -> p t c", c=128), [ao_], [D_attnT], ao_)

                for (h, qb, gr) in groups:
                    if qb == 0 and gr == 0:
                        while pend_prep:
                            pend_prep.pop(0)()
                        if h + 1 < NH:
                            pend_prep = prep_pieces(h + 1)
                    q_ = qT[h % 2]; k_ = kT[h % 2]
                    p = pSs[gi % 2]; pt_ = PT[gi % NPT]
                    for u in range(2):
                        kt = gr * 2 + u
                        mm(p[:, u * 512:(u + 1) * 512], k_[:, kt * 128:(kt + 1) * 128], q_[:, qb * 512:(qb + 1) * 512], True, True, [k_, q_], [p], sig=(u == 1))
                    act(lambda e, p=p, pt_=pt_: e.activation(out=pt_[:], in_=p[:], func=AF.Exp, scale=SCALE), [p], [pt_])
                    if pending is not None:
                        emit_pv(pending)
                    pending = (h, qb, gr, pt_)
                    gi += 1
                    if pend_prep and gi % 3 == 0:
                        pend_prep.pop(0)()
                emit_pv(pending)
                sc.barrier()

            bs.close()
            fs = contextlib.ExitStack()
            es.enter_context(fs)
            h2T = sb(fs, "h2T", [128, 8, S], BF16)
            h2Tk = [Buf(h2T.t, "h2T%d" % k) for k in range(8)]
            with contextlib.ExitStack() as st:
                woa = sb(st, "woa", [128, 8, D], BF16)
                wos = sb(st, "wos", [128, 8, D], BF16)
                aT = [sb(st, "aT%d" % i, [128, 8, 128], BF16) for i in range(3)]
                sT = [sb(st, "sT%d" % i, [128, 8, 128], BF16) for i in range(3)]
                xt = [sb(st, "xt5_%d" % i, [128, D], F32) for i in range(3)]
                x1 = [sb(st, "x1_%d" % i, [128, D], F32) for i in range(2)]
                tq = sb(st, "tq", [128, D], F32)
                xs2 = [sb(st, "xs2_%d" % i, [128, D], BF16) for i in range(2)]
                junk5 = sb(st, "junk5", [128, D], BF16)
                junk6 = sb(st, "junk6", [128, D], BF16)
                tmp5 = [sb(st, "tmp5_%d" % i, [128, 8, 128], BF16) for i in range(2)]
                ss5 = [sb(st, "ss5_%d" % i, [128, 2], F32) for i in range(2)]
                ss6 = [sb(st, "ss6_%d" % i, [128, 2], F32) for i in range(2)]
                pM = [ps(st, "pM%d" % i, [128, 1024], F32) for i in range(2)]
                pT5 = [ps(st, "pT5_%d" % i, [128, 8, 128], BF16) for i in range(2)]
                GW2 = sb(st, "GW2", [128, D], F32)
                rw = sb(st, "rw5", [128, D], F32)
                sc.dma('sp', rw[:], rows_d[0:1, :].partition_broadcast(128), [], [rw], rw)
                sc.dma('sp', GW2[:], modv_d[b:b + 1, 2 * D:3 * D].partition_broadcast(128), [D_modv], [GW2], GW2)
                dve(lambda e: e.tensor_tensor(out=GW2[:], in0=GW2[:], in1=rw[:], op=ALU.mult), [GW2, rw], [GW2])
                sc.dma('sp', woa[:], w_out_b[0:1024, :].rearrange("(k p) c -> p k c", p=128), [D_w["w_out_b"]], [woa], woa)
                sc.dma('sp', wos[:], w_out_b[1024:2048, :].rearrange("(k p) c -> p k c", p=128), [D_w["w_out_b"]], [wos], wos)

                def p5_loads(t):
                    a_ = aT[t % 3]; s_ = sT[t % 3]; x_ = xt[t % 3]
                    sc.dma('sp', a_[:].rearrange("p h t -> p (h t)"), attnT_d[t], [D_attnT], [a_], a_)
                    sc.dma('sp', s_[:].rearrange("p j t -> p (j t)"), ssdnT_d[t], [D_ssdnT], [s_], s_)
                    sc.dma('sp', x_[:], x_d[b, t * 128:(t + 1) * 128, :], [], [x_], x_)

                p5_loads(0)

                def p5_front(t):
                    a_ = aT[t % 3]; s_ = sT[t % 3]; x_ = xt[t % 3]; x1_ = x1[t % 2]; p = pM[t % 2]
                    s5 = ss5[t % 2]; s6 = ss6[t % 2]; xs_ = xs2[t % 2]
                    if t + 1 < NCL:
                        p5_loads(t + 1)
                    for nb in range(2):
                        for hh in range(8):
                            mm(p[:, nb * 512:(nb + 1) * 512], a_[:, hh, :], woa[:, hh, nb * 512:(nb + 1) * 512], hh == 0, False, [a_, woa], [p], sig=False)
                        for k in range(8):
                            mm(p[:, nb * 512:(nb + 1) * 512], s_[:, k, :], wos[:, k, nb * 512:(nb + 1) * 512], False, k == 7, [s_, wos], [p], sig=(k == 7))
                    act(lambda e: e.activation(out=junk5[:], in_=p[:], func=AF.Square, accum_out=s5[:, 0:1]), [p], [junk5, s5])
                    rstd_from_ss(s5, D)
                    dve(lambda e: e.scalar_tensor_tensor(out=tq[:], in0=p[:], scalar=s5[:, 1:2], in1=GW2[:], op0=ALU.mult, op1=ALU.mult), [p, s5, GW2], [tq])
                    pool(lambda e: e.tensor_tensor(out=x1_[:], in0=tq[:], in1=x_[:], op=ALU.add), [tq, x_], [x1_])
                    sc.dma('sp', x1_d[t * 128:(t + 1) * 128, :], x1_[:], [x1_], [D_x1], x1_)
                    act(lambda e: e.activation(out=junk6[:], in_=x1_[:], func=AF.Square, accum_out=s6[:, 0:1]), [x1_], [junk6, s6])
                    rstd_from_ss(s6, D)
                    dve(lambda e: e.tensor_scalar(out=xs_[:], in0=x1_[:], scalar1=s6[:, 1:2], scalar2=None, op0=ALU.mult), [x1_, s6], [xs_])

                def p5_back(t):
                    xs_ = xs2[t % 2]; pt = pT5[t % 2]
                    for k in range(8):
                        tr(pt[:, k, :], xs_[:, k * 128:(k + 1) * 128], identb[:], [xs_, identb], [pt], sig=(k == 7))
                    tm_ = tmp5[t % 2]
                    dve(lambda e: e.tensor_tensor(out=tm_[:], in0=pt[:], in1=G4[:].unsqueeze(2).to_broadcast([128, 8, 128]), op=ALU.mult), [pt, G4], [tm_])
                    pool(lambda e: e.tensor_tensor(out=h2T[:, :, t * 128:(t + 1) * 128], in0=tm_[:], in1=modT[0][:, 24:32].unsqueeze(2).to_broadcast([128, 8, 128]), op=ALU.add),
                         [tm_, modT[0]], h2Tk)

                for t in range(NCL + 1):
                    if t < NCL:
                        p5_front(t)
                    if t >= 1:
                        p5_back(t - 1)
                sc.barrier()

            actT = sb(fs, "actT", [128, NFF, S], BF16)
            actTj = [Buf(actT.t, "actT%d" % j) for j in range(NFF)]
            with contextlib.ExitStack() as st:
                wg = [sb(st, "wg%d" % i, [128, 8, 128], BF16) for i in range(2)]
                wv_ = [sb(st, "wvv%d" % i, [128, 8, 128], BF16) for i in range(2)]
                Gt = [sb(st, "Gt%d" % i, [128, S + 2], F32) for i in range(2)]
                Vv = [sb(st, "Vv%d" % i, [128, S], BF16) for i in range(2)]
                cv = [sb(st, "cv%d" % i, [128, S], F32) for i in range(2)]
                ge = [sb(st, "ge%d" % i, [128, S], BF16) for i in range(2)]
                pG = [ps(st, "pG%d" % i, [128, 512], F32) for i in range(4)]
                pV = [ps(st, "pV%d" % i, [128, 512], F32) for i in range(4)]
                for i in range(2):
                    dve(lambda e, i=i: e.memset(Gt[i][:], 0.0), [], [Gt[i]])
                pidx = 0
                for j in range(NFF):
                    g_ = wg[j % 2]; v_ = wv_[j % 2]; G_ = Gt[j % 2]; V_ = Vv[j % 2]; c_ = cv[j % 2]; e_ = ge[j % 2]
                    sc.dma('sp', g_[:], w_up_b[:, j * 128:(j + 1) * 128].rearrange("(k p) c -> p k c", p=128), [D_w["w_up_b"]], [g_], g_)
                    sc.dma('sp', v_[:], w_up_b[:, DFF + j * 128:DFF + (j + 1) * 128].rearrange("(k p) c -> p k c", p=128), [D_w["w_up_b"]], [v_], v_)
                    for tb in range(S // 512):
                        pg = pG[pidx % 4]; pv_ = pV[pidx % 4]; pidx += 1
                        for k in range(8):
                            mm(pg[:], g_[:, k, :], h2T[:, k, tb * 512:(tb + 1) * 512], k == 0, k == 7, [g_, h2Tk[k]], [pg])
                        for k in range(8):
                            mm(pv_[:], v_[:, k, :], h2T[:, k, tb * 512:(tb + 1) * 512], k == 0, k == 7, [v_, h2Tk[k]], [pv_])
                        act(lambda e, pg=pg, G_=G_, tb=tb: e.activation(out=G_[:, 1 + tb * 512:1 + (tb + 1) * 512], in_=pg[:], func=AF.Copy), [pg], [G_])
                        dve(lambda e, pv_=pv_, V_=V_, tb=tb: e.tensor_copy(out=V_[:, tb * 512:(tb + 1) * 512], in_=pv_[:]), [pv_], [V_])
                    en = 'dve'
                    en2 = 'pool' if j % 2 == 0 else 'dve'
                    w0 = colv[:, R_FCW + j:R_FCW + j + 1]; w1 = colv[:, R_FCW + 22 + j:R_FCW + 23 + j]; w2 = colv[:, R_FCW + 44 + j:R_FCW + 45 + j]
                    bc = colv[:, R_FCB + j:R_FCB + j + 1]
                    act(lambda e, G_=G_, c_=c_, w0=w0, bc=bc: e.activation(out=c_[:], in_=G_[:, 0:S], func=AF.Identity, scale=w0, bias=bc), [G_, colv], [c_])
                    sc.op(en, lambda e, G_=G_, c_=c_, w1=w1: e.scalar_tensor_tensor(out=c_[:], in0=G_[:, 1:S + 1], scalar=w1, in1=c_[:], op0=ALU.mult, op1=ALU.add), [G_, colv, c_], [c_])
                    sc.op(en, lambda e, G_=G_, c_=c_, w2=w2: e.scalar_tensor_tensor(out=c_[:], in0=G_[:, 2:S + 2], scalar=w2, in1=c_[:], op0=ALU.mult, op1=ALU.add), [G_, colv, c_], [c_])
                    act(lambda e, c_=c_, e_=e_: e.activation(out=e_[:], in_=c_[:], func=AF.Gelu), [c_], [e_])
                    sc.op(en2, lambda e, e_=e_, V_=V_, j=j: e.tensor_tensor(out=actT[:, j, :], in0=e_[:], in1=V_[:], op=ALU.mult), [e_, V_], [actTj[j]])
                sc.barrier()

            with contextlib.ExitStack() as st:
                wd = sb(st, "wd", [128, NFF, D], BF16)
                x1 = [sb(st, "x1r%d" % i, [128, D], F32) for i in range(2)]
                ot = [sb(st, "ot%d" % i, [128, D], F32) for i in range(2)]
                tq = sb(st, "tq7", [128, D], F32)
                junk7 = sb(st, "junk7", [128, D], BF16)
                ss7 = sb(st, "ss7", [128, 2], F32)
                pM = [ps(st, "pM7_%d" % i, [128, 1024], F32) for i in range(2)]
                GW5 = sb(st, "GW5", [128, D], F32)
                rw = sb(st, "rw7", [128, D], F32)
                sc.dma('sp', rw[:], rows_d[1:2, :].partition_broadcast(128), [], [rw], rw)
                sc.dma('sp', GW5[:], modv_d[b:b + 1, 5 * D:6 * D].partition_broadcast(128), [D_modv], [GW5], GW5)
                dve(lambda e: e.tensor_tensor(out=GW5[:], in0=GW5[:], in1=rw[:], op=ALU.mult), [GW5, rw], [GW5])
                sc.dma('sp', wd[:], w_down_b.rearrange("(k p) c -> p k c", p=128), [D_w["w_down_b"]], [wd], wd)
                for t in range(NCL):
                    x1_ = x1[t % 2]; o_ = ot[t % 2]; p = pM[t % 2]
                    sc.dma('sp', x1_[:], x1_d[t * 128:(t + 1) * 128, :], [D_x1], [x1_], x1_)
                    for nb in range(2):
                        for j in range(NFF):
                            mm(p[:, nb * 512:(nb + 1) * 512], actT[:, j, t * 128:(t + 1) * 128], wd[:, j, nb * 512:(nb + 1) * 512], j == 0, j == NFF - 1, [actTj[j], wd], [p])
                    act(lambda e, p=p: e.activation(out=junk7[:], in_=p[:], func=AF.Square, accum_out=ss7[:, 0:1]), [p], [junk7, ss7])
                    rstd_from_ss(ss7, D)
                    dve(lambda e, p=p: e.scalar_tensor_tensor(out=tq[:], in0=p[:], scalar=ss7[:, 1:2], in1=GW5[:], op0=ALU.mult, op1=ALU.mult), [p, ss7, GW5], [tq])
                    pool(lambda e, x1_=x1_, o_=o_: e.tensor_tensor(out=o_[:], in0=tq[:], in1=x1_[:], op=ALU.add), [tq, x1_], [o_])
                    sc.dma('sp', out_d[b, t * 128:(t + 1) * 128, :], o_[:], [o_], [D_out], o_)
                sc.barrier()
            fs.close()
            ms.close()
        sc.barrier()
    return nc


def _rope_tables(S, CTX):
    GRID_W = 64
    n_rows = S // GRID_W
    row = np.repeat(np.arange(n_rows), GRID_W).astype(np.float32)
    col = np.tile(np.arange(GRID_W), n_rows).astype(np.float32)
    axis_dim = 16
    inv_freq = (10000.0 ** (-np.arange(0, axis_dim, 2, dtype=np.float32) / axis_dim)).astype(np.float32)
    ang_r = row[:, None] * inv_freq
    ang_c = col[:, None] * inv_freq
    ang = np.concatenate([ang_r, ang_r, ang_c, ang_c], axis=-1).astype(np.float32)
    cos, sin = np.cos(ang), np.sin(ang)
    sgn = np.array([-1.0] * 8 + [1.0] * 8 + [-1.0] * 8 + [1.0] * 8, np.float32)
    L = CTX + S
    c1 = np.ones((96, L), np.float32)
    c2 = np.zeros((96, L), np.float32)
    c1[64:96, CTX:] = cos.T
    c2[64:96, CTX:] = (sin * sgn).T
    return c1, c2


_SWAP = np.concatenate([np.arange(8, 16), np.arange(0, 8), np.arange(24, 32), np.arange(16, 24)])


def prepare_shared(inp, S, CTX):
    f = lambda a: np.ascontiguousarray(np.asarray(a, dtype=np.float32))
    w_in = f(inp["w_in"][0])
    kr = w_in[:, 640:672]
    wkr = np.zeros((D, 192), np.float32)
    wkr[:, 64:96] = kr
    wkr[:, 160:192] = kr[:, _SWAP]
    wq = f(inp["w_q_up"][0])
    wq3 = wq.reshape(384, NH, 96)
    wqs = np.zeros_like(wq3)
    wqs[:, :, 64:96] = wq3[:, :, 64:96][:, :, _SWAP]
    wkv = f(inp["w_kv_up"][0]).reshape(256, NH, 128)
    wk = np.ascontiguousarray(wkv[:, :, 0:64]).reshape(256, NH * 64)
    wv = np.ascontiguousarray(wkv[:, :, 64:128]).reshape(256, NH * 64)
    vecs = np.zeros((256, 128), np.float32)
    vecs[0:8] = f(inp["mix_pre_norm"][0]).reshape(8, 128)
    vecs[8:11] = f(inp["q_norm"][0]).reshape(3, 128)
    vecs[11:13] = f(inp["kv_norm"][0]).reshape(2, 128)
    vecs[13:73] = f(inp["ssd_conv_w"][0]).reshape(60, 128)
    vecs[73:85] = f(inp["ssd_conv_b"][0]).reshape(12, 128)
    vecs[85:93] = f(inp["ssd_norm"][0]).reshape(8, 128)
    vecs[93:101] = f(inp["ffn_pre_norm"][0]).reshape(8, 128)
    vecs[101:167] = f(inp["ffn_conv_w"][0]).reshape(66, 128)
    vecs[167:189] = f(inp["ffn_conv_b"][0]).reshape(22, 128)
    vecs[189:197] = np.repeat(f(inp["ssd_d"][0]), 64).reshape(8, 128)
    rows = np.zeros((4, D), np.float32)
    rows[0] = f(inp["mix_post_norm"][0])
    rows[1] = f(inp["ffn_post_norm"][0])
    small = np.zeros((1, 96), np.float32)
    small[0, 0:32] = f(inp["ssd_a_log"][0]).reshape(32)
    small[0, 32:64] = f(inp["ssd_dt_bias"][0]).reshape(32)
    small[0, 64:80] = f(inp["ssd_d"][0])
    c1, c2 = _rope_tables(S, CTX)
    ii = np.arange(128)
    return dict(
        w_mod=f(inp["w_mod"][0]), b_mod=f(inp["b_mod"][0]).reshape(1, -1), w_in=w_in, wkr=wkr,
        wq=np.ascontiguousarray(wq), wqs=np.ascontiguousarray(wqs.reshape(384, NH * 96)), wk=wk, wv=wv,
        w_out=f(inp["w_out"][0]), w_up=f(inp["w_up"][0]), w_down=f(inp["w_down"][0]),
        vecs=vecs, rows=rows, small=small, ident=np.eye(128, dtype=np.float32),
        uf=(ii[:, None] <= ii[None, :]).astype(np.float32), ub=(ii[:, None] >= ii[None, :]).astype(np.float32),
        c1=c1, c2=c2)


def make_in_maps(inp, n_cores, NB):
    x = np.asarray(inp["x"], np.float32)
    ctx = np.asarray(inp["ctx"], np.float32)
    c = np.asarray(inp["c"], np.float32)
    c_ctx = np.asarray(inp["c_ctx"], np.float32).reshape(1, D)
    S, CTX = x.shape[1], ctx.shape[1]
    shared = prepare_shared(inp, S, CTX)
    maps = []
    for i in range(n_cores):
        m = dict(shared)
        m["x"] = np.ascontiguousarray(x[i * NB:(i + 1) * NB])
        m["ctx"] = np.ascontiguousarray(ctx[i * NB:(i + 1) * NB])
        m["cc"] = np.ascontiguousarray(np.concatenate([c[i * NB:(i + 1) * NB], c_ctx], axis=0))
        maps.append(m)
    return maps


def kernel(**inputs):
    n_cores = 8
    x = np.asarray(inputs["x"])
    B, S, _ = x.shape
    CTX = np.asarray(inputs["ctx"]).shape[1]
    NB = B // n_cores
    nc = build_nc(NB, S, CTX)
    in_maps = make_in_maps(inputs, n_cores, NB)
    res = run_bass_kernel_spmd(nc, in_maps, core_ids=list(range(n_cores)))
    out = np.concatenate([np.asarray(r["out"], dtype=np.float32) for r in res.results], axis=0)
    return out
```
